# Optimizing a Trainium2 kernel written in Bass

```python
import jax
import jax.numpy as jnp
from jax import lax
import numpy as np

D_MODEL = 1024
BATCH = 1
SEQ = 16384
DEPTH = 1
DEC_BATCH = 4
DEC_SEQ = 4096
PAST_LEN = 128

GRID_W = 64
Q_BLOCK = 128
ROPE_THETA = 10000.0
EPS = 1e-6

A_HEADS = 8
A_KV_HEADS = 2
A_HEAD_DIM = 64
A_WIDTH = A_HEADS * A_HEAD_DIM

B_HEADS = 8
B_NOPE = 64
B_ROPE = 32
B_V = 64
B_Q_RANK = 384
B_KV_RANK = 256
B_WIDTH = B_HEADS * B_V

IN_SIZES = (A_HEADS * A_HEAD_DIM, A_KV_HEADS * A_HEAD_DIM, A_KV_HEADS * A_HEAD_DIM, A_WIDTH,
            B_Q_RANK, B_KV_RANK, B_ROPE, B_WIDTH, D_MODEL, D_MODEL)
IN_WIDTH = sum(IN_SIZES)

kernel_name = 'hybrid_gqa_mla_encoder'


def _split_points():
    pts, acc = [], 0
    for s in IN_SIZES[:-1]:
        acc += s
        pts.append(acc)
    return pts


def rms_norm(x, g):
    xf = x.astype(jnp.float32)
    y = xf * lax.rsqrt(jnp.mean(xf * xf, axis=-1, keepdims=True) + EPS)
    return (y * g.astype(jnp.float32)).astype(x.dtype)


def axial_rope_tables(n, d_rot):
    rows = n // GRID_W
    row_ids = jnp.repeat(jnp.arange(rows, dtype=jnp.float32), GRID_W)
    col_ids = jnp.tile(jnp.arange(GRID_W, dtype=jnp.float32), rows)
    d_axis = d_rot // 2
    inv = ROPE_THETA ** (-jnp.arange(0, d_axis, 2, dtype=jnp.float32) / d_axis)
    ang = jnp.concatenate([row_ids[:, None] * inv, col_ids[:, None] * inv], axis=-1)
    return jnp.cos(ang), jnp.sin(ang)


def apply_rope(x, cos, sin):
    half = x.shape[-1] // 2
    xf = x.astype(jnp.float32)
    x1, x2 = xf[..., :half], xf[..., half:]
    c = cos[None, :, None, :]
    s = sin[None, :, None, :]
    return jnp.concatenate([x1 * c - x2 * s, x2 * c + x1 * s], axis=-1).astype(x.dtype)


def block_attention(q, k, v, scale):
    bsz, n, h, dk = q.shape
    g = k.shape[2]
    r = h // g
    nb = n // Q_BLOCK
    qb = q.reshape(bsz, nb, Q_BLOCK, g, r, dk).transpose(1, 0, 2, 3, 4, 5)

    def one_block(qblk):
        s = jnp.einsum('bqgrd,bkgd->bgrqk', qblk, k, preferred_element_type=jnp.float32) * scale
        p = jax.nn.softmax(s, axis=-1).astype(v.dtype)
        return jnp.einsum('bgrqk,bkgd->bqgrd', p, v)

    o = lax.map(one_block, qb)
    return o.transpose(1, 0, 2, 3, 4, 5).reshape(bsz, n, h, v.shape[-1])


def encoder_layer(x, c, cos_a, sin_a, cos_b, sin_b, ada_w, ada_b, pre_g, post_g, w_in,
                  a_q_g, a_k_g, b_q_g, b_q_up, b_kv_g, b_kv_up, a_out, b_out, w_o):
    bsz, n, _ = x.shape
    mod = jax.nn.silu(c) @ ada_w + ada_b
    shift, scale, gate = jnp.split(mod[:, None, :], 3, axis=-1)
    h = rms_norm(x, pre_g) * (1.0 + scale) + shift
    aq, ak, av, ag, bq, bkv, bkr, bg, ma, mb = jnp.split(h @ w_in, _split_points(), axis=-1)

    qa = apply_rope(rms_norm(aq.reshape(bsz, n, A_HEADS, A_HEAD_DIM), a_q_g), cos_a, sin_a)
    ka = apply_rope(rms_norm(ak.reshape(bsz, n, A_KV_HEADS, A_HEAD_DIM), a_k_g), cos_a, sin_a)
    va = av.reshape(bsz, n, A_KV_HEADS, A_HEAD_DIM)
    ya = block_attention(qa, ka, va, A_HEAD_DIM ** -0.5).reshape(bsz, n, A_WIDTH) * jax.nn.silu(ag)

    qb = (rms_norm(bq, b_q_g) @ b_q_up).reshape(bsz, n, B_HEADS, B_NOPE + B_ROPE)
    qb = jnp.concatenate([qb[..., :B_NOPE], apply_rope(qb[..., B_NOPE:], cos_b, sin_b)], axis=-1)
    kv = (rms_norm(bkv, b_kv_g) @ b_kv_up).reshape(bsz, n, B_HEADS, B_NOPE + B_V)
    kr = apply_rope(bkr.reshape(bsz, n, 1, B_ROPE), cos_b, sin_b)
    kb = jnp.concatenate([kv[..., :B_NOPE], jnp.broadcast_to(kr, (bsz, n, B_HEADS, B_ROPE))], axis=-1)
    vb = kv[..., B_NOPE:]
    yb = block_attention(qb, kb, vb, (B_NOPE + B_ROPE) ** -0.5).reshape(bsz, n, B_WIDTH) * jax.nn.silu(bg)

    merged = jax.nn.sigmoid(ma) * (ya @ a_out) + jax.nn.sigmoid(mb) * (yb @ b_out)
    return x + gate * rms_norm(merged @ w_o, post_g)


def trunk(x, c, ada_w, ada_b, pre_norm_g, post_norm_g, w_in, a_q_norm_g, a_k_norm_g,
          b_q_norm_g, b_q_up, b_kv_norm_g, b_kv_up, a_out, b_out, w_o):
    n = x.shape[1]
    cos_a, sin_a = axial_rope_tables(n, A_HEAD_DIM)
    cos_b, sin_b = axial_rope_tables(n, B_ROPE)
    for l in range(DEPTH):
        x = encoder_layer(x, c, cos_a, sin_a, cos_b, sin_b, ada_w[l], ada_b[l], pre_norm_g[l],
                          post_norm_g[l], w_in[l], a_q_norm_g[l], a_k_norm_g[l], b_q_norm_g[l],
                          b_q_up[l], b_kv_norm_g[l], b_kv_up[l], a_out[l], b_out[l], w_o[l])
    return x


def setup_inputs(seed: int = 0) -> dict:
    key = jax.random.key(seed)
    ks = jax.random.split(key, 20)
    f32 = jnp.float32

    def nrm(k, shape, s):
        return jax.random.normal(k, shape, f32) * s

    def gain(k, d):
        return 1.0 + 0.02 * jax.random.normal(k, (DEPTH, d), f32)

    return {
        'x_prompt': jax.random.normal(ks[0], (BATCH, SEQ, D_MODEL), f32),
        'x_sample': jax.random.normal(ks[1], (DEC_BATCH, DEC_SEQ, D_MODEL), f32),
        'c_prompt': jax.random.normal(ks[2], (BATCH, D_MODEL), f32),
        'c_sample': jax.random.normal(ks[3], (DEC_BATCH, D_MODEL), f32),
        'ada_w': nrm(ks[4], (DEPTH, D_MODEL, 3 * D_MODEL), 0.2 * D_MODEL ** -0.5),
        'ada_b': nrm(ks[5], (DEPTH, 3 * D_MODEL), 0.02),
        'pre_norm_g': gain(ks[6], D_MODEL),
        'post_norm_g': gain(ks[7], D_MODEL),
        'w_in': nrm(ks[8], (DEPTH, D_MODEL, IN_WIDTH), D_MODEL ** -0.5),
        'a_q_norm_g': gain(ks[9], A_HEAD_DIM),
        'a_k_norm_g': gain(ks[10], A_HEAD_DIM),
        'b_q_norm_g': gain(ks[11], B_Q_RANK),
        'b_q_up': nrm(ks[12], (DEPTH, B_Q_RANK, B_HEADS * (B_NOPE + B_ROPE)), B_Q_RANK ** -0.5),
        'b_kv_norm_g': gain(ks[13], B_KV_RANK),
        'b_kv_up': nrm(ks[14], (DEPTH, B_KV_RANK, B_HEADS * (B_NOPE + B_V)), B_KV_RANK ** -0.5),
        'a_out': nrm(ks[15], (DEPTH, A_WIDTH, D_MODEL), A_WIDTH ** -0.5),
        'b_out': nrm(ks[16], (DEPTH, B_WIDTH, D_MODEL), B_WIDTH ** -0.5),
        'w_o': nrm(ks[17], (DEPTH, D_MODEL, D_MODEL), D_MODEL ** -0.5),
    }


def reference(x_prompt, x_sample, c_prompt, c_sample, ada_w, ada_b, pre_norm_g, post_norm_g,
              w_in, a_q_norm_g, a_k_norm_g, b_q_norm_g, b_q_up, b_kv_norm_g, b_kv_up,
              a_out, b_out, w_o):
    y_prompt = trunk(x_prompt, c_prompt, ada_w, ada_b, pre_norm_g, post_norm_g, w_in,
                     a_q_norm_g, a_k_norm_g, b_q_norm_g, b_q_up, b_kv_norm_g, b_kv_up,
                     a_out, b_out, w_o)
    y_sample = trunk(x_sample, c_sample, ada_w, ada_b, pre_norm_g, post_norm_g, w_in,
                     a_q_norm_g, a_k_norm_g, b_q_norm_g, b_q_up, b_kv_norm_g, b_kv_up,
                     a_out, b_out, w_o)
    return (y_prompt, y_sample)
```

```python
import numpy as np
from contextlib import ExitStack
from functools import partial
import concourse.bass as bass
import concourse.mybir as mybir
from concourse.bass_utils import run_bass_kernel_spmd

F32 = mybir.dt.float32
BF16 = mybir.dt.bfloat16
ALU = mybir.AluOpType
AF = mybir.ActivationFunctionType

D = 1024
EPS = 1e-6
GRID_W = 64
ROPE_THETA = 10000.0
C_AQ, C_AK, C_AV, C_AG, C_BQ, C_BKV, C_BKR, C_BG, C_MA, C_MB = 0, 512, 640, 768, 1280, 1664, 1920, 1952, 2464, 3488
TB = 512
SKEW_A = True
HOIST_FE = True
RSQRT_LN = False
SAME_SYNC = True


class Res:
    __slots__ = ("name", "w", "rs")

    def __init__(self, name):
        self.name = name
        self.w = None
        self.rs = {}


class Prog:
    def __init__(self, nc, stack):
        self.nc = nc
        self.stack = stack
        self.ops = []

    def op(self, eng, name, kw, reads=(), writes=(), dma_key=None):
        idx = len(self.ops)
        ek = ("d", dma_key) if dma_key else ("e", eng)
        deps = set()
        for r in reads:
            if r.w is not None:
                deps.add(r.w)
        for w in writes:
            if w.w is not None:
                deps.add(w.w)
            deps.update(w.rs.values())
        for r in reads:
            r.rs[ek] = idx
        for w in writes:
            w.w = idx
            w.rs = {}
        deps.discard(idx)
        ndma = len(kw) if isinstance(kw, list) else 1
        self.ops.append(dict(eng=eng, name=name, kw=kw, deps=deps, ek=ek, ndma=ndma, sig=False, val=0))
        return idx

    def _skip(self, ek, eng):
        return ek[0] == "e" and ek[1] == eng and (eng == "pe" or not SAME_SYNC)

    def finalize(self):
        ops = self.ops
        for o in ops:
            red = {}
            for d in o["deps"]:
                ek = ops[d]["ek"]
                if ek not in red or red[ek] < d:
                    red[ek] = d
            o["rdeps"] = {ek: d for ek, d in red.items() if not self._skip(ek, o["eng"])}
            for d in o["rdeps"].values():
                ops[d]["sig"] = True
        cnt = {}
        for o in ops:
            ek = o["ek"]
            if ek[0] == "d":
                cnt[ek] = cnt.get(ek, 0) + 16 * o["ndma"]
                o["val"] = cnt[ek]
            elif o["sig"]:
                cnt[ek] = cnt.get(ek, 0) + 1
                o["val"] = cnt[ek]
        self.sems = {}
        for i, ek in enumerate(cnt):
            self.sems[ek] = self.stack.enter_context(self.nc.semaphore("s%d_%s" % (i, str(ek[1])[:12])))

    def replay(self, engname, eng):
        ops = self.ops
        seen = {}
        for o in ops:
            if o["eng"] != engname:
                continue
            for ek, d in o["rdeps"].items():
                v = ops[d]["val"]
                if seen.get(ek, 0) >= v:
                    continue
                eng.wait_ge(self.sems[ek], v)
                seen[ek] = v
            f = getattr(eng, o["name"])
            if isinstance(o["kw"], list):
                insts = [f(**k) for k in o["kw"]]
            else:
                insts = [f(**o["kw"])]
            if o["ek"][0] == "d":
                for ins in insts:
                    ins.then_inc(self.sems[o["ek"]], 16)
            elif o["sig"]:
                insts[0].then_inc(self.sems[o["ek"]], 1)

    def final_wait(self, eng):
        last = {}
        for o in self.ops:
            if o["ek"][0] == "d":
                last[o["ek"]] = o["val"]
        for ek, v in last.items():
            eng.wait_ge(self.sems[ek], v)


def _build(NC, QP, QS, segs=((0, 0, None), (1, 0, None))):
    NP = NC * QP
    NS = 2 * QS
    seqs = [dict(n=NP, q=QP), dict(n=NS, q=QS)]
    CK = min(1024, NP, NS)
    TPC = CK // 128
    for s in seqs:
        s["nch"] = s["n"] // CK
        assert s["n"] % CK == 0 and s["q"] % TB == 0 and s["n"] % TB == 0

    nc = bass.Bass("TRN2", target_bir_lowering=False)
    stack = ExitStack()
    P = Prog(nc, stack)

    def din(name, shape, dt=F32):
        return nc.dram_tensor(name, list(shape), dt, kind="ExternalInput").ap()

    def dscr(name, shape, dt=BF16):
        return nc.dram_tensor(name, list(shape), dt, kind="Internal").ap()

    xk = [din("xk0", [NP, D]), din("xk1", [NS, D])]
    tabA = [din("tabA0", [2, 64, NP]), din("tabA1", [2, 64, NS])]
    tabB = [din("tabB0", [2, 32, NP]), din("tabB1", [2, 32, NS])]
    vecs_d = din("vecs", [128, 64])
    rows_d = din("rows", [1, 2048])
    consts_d = din("consts", [128, 3, 128])
    ada_w = din("ada_w", [D, 3 * D])
    w_in = din("w_in", [D, 4512])
    b_q_up = din("b_q_up", [384, 768])
    b_kv_up = din("b_kv_up", [256, 1024])
    a_out = din("a_out", [512, D])
    b_out = din("b_out", [512, D])
    w_o = din("w_o", [D, D])
    outs = [nc.dram_tensor("y0", [QP, D], F32, kind="ExternalOutput").ap(),
            nc.dram_tensor("y1", [QS, D], F32, kind="ExternalOutput").ap()]

    NG = 13
    wst = dscr("wst", [NG, 128, 4096])
    r_wst = [Res("wst%d" % g) for g in range(NG)]
    for s, sq in enumerate(seqs):
        n = sq["n"]
        sq["KA"] = dscr("KA%d" % s, [128, n])
        sq["KBn"] = dscr("KBn%d" % s, [4, 128, n])
        sq["KR"] = dscr("KR%d" % s, [32, n])
        sq["VA"] = dscr("VA%d" % s, [2, sq["nch"], 128, TPC, 64])
        sq["VB"] = dscr("VB%d" % s, [8, sq["nch"], 128, TPC, 64])

    def sb(name, shape, dt):
        return stack.enter_context(nc.sbuf_tensor(name, list(shape), dt))

    ARENA = 85 * 1024
    arena = sb("arena", [128, ARENA // 2], BF16)

    class Carver:
        def __init__(self, off=0):
            self.off = off
            self.hi = off

        def get(self, shape, dt, parts=128):
            esz = 4 if dt == F32 else 2
            nel = 1
            for d_ in shape[1:]:
                nel *= d_
            nb = (nel * esz + 31) // 32 * 32
            v = arena[:, self.off // 2:(self.off + nel * esz) // 2]
            if dt == F32:
                v = v.bitcast(F32)
            if len(shape) == 3:
                v = v.rearrange("p (a b) -> p a b", b=shape[2])
            elif len(shape) == 4:
                v = v.rearrange("p (a b c) -> p a b c", b=shape[2], c=shape[3])
            self.off += nb
            assert self.off <= ARENA, self.off
            return v

    ps = stack.enter_context(nc.psum_tensor("ps", [128, 8, 512], F32))
    psr = [Res("ps%d" % b) for b in range(8)]
    bank_rr = [0]

    def bank():
        b = bank_rr[0]
        bank_rr[0] = (b + 1) % 8
        return b

    consts_b = sb("consts_b", [128, 3, 128], BF16)
    r_consts = Res("consts")
    ident = consts_b[:, 0, :]
    ones_b = consts_b[:, 1, :]
    bones_b = consts_b[:, 2, :]
    vecs = sb("vecs_sb", [128, 64], F32)
    r_vecs = Res("vecs")
    V_PREG, V_ADAB, V_GAQ, V_GAQR, V_GAK, V_GAKR, V_BQG, V_BKVG, V_CT = 0, 8, 32, 33, 34, 35, 36, 39, 41
    scT = sb("scT", [128, 8, 2], F32)
    aT = sb("aT", [128, 2, 8], F32)
    shT = sb("shT", [128, 2, 8], F32)
    r_mod = Res("mod")
    gp_bc = sb("gp_bc", [128, 2, 1024], F32)
    r_gp = Res("gp")
    wqup_sb = sb("wqup_sb", [128, 3, 1536], BF16)
    r_wqup = Res("wqup")
    NWB = 3
    wbuf = [sb("wbuf%d" % i, [128, 4096], BF16) for i in range(NWB)]
    r_wbuf = [Res("wbuf%d" % i) for i in range(NWB)]
    wb_rr = [0]
    NXT = 3
    xt = [sb("xt%d" % i, [128, D], F32) for i in range(NXT)]
    r_xt = [Res("xt%d" % i) for i in range(NXT)]
    xt_rr = [0]
    xn = [sb("xn%d" % i, [128, D], BF16) for i in range(2)]
    r_xn = [Res("xn%d" % i) for i in range(2)]
    xn_rr = [0]
    junk = sb("junk", [128, D], BF16)
    r_junk = Res("junk")
    ss = sb("ss", [128, 8], F32)
    r_ss = [Res("ss%d" % i) for i in range(8)]
    ss_rr = [0]
    st1 = sb("st1", [128, 8], F32)
    r_st1 = [Res("st1_%d" % i) for i in range(8)]
    st_rr = [0]
    hT = [sb("hT%d" % i, [128, 8, TB], BF16) for i in range(2)]
    r_hT = [[Res("hT%d_%d" % (i, c)) for c in range(8)] for i in range(2)]
    hT_rr = [0]
    tA = [sb("tA%d" % i, [128, 2, TB], F32) for i in range(2)]
    r_tA = [Res("tA%d" % i) for i in range(2)]
    tab_rr = [0]
    NTMP = 8
    tmp = [sb("tmp%d" % i, [128, TB], F32) for i in range(NTMP)]
    r_tmp = [Res("tmp%d" % i) for i in range(NTMP)]
    tmp_rr = [0]

    def gettmp():
        i = tmp_rr[0]
        tmp_rr[0] = (i + 1) % NTMP
        return tmp[i], r_tmp[i]

    NSQ = 4
    sqb = [sb("sqb%d" % i, [128, TB], BF16) for i in range(NSQ)]
    r_sqb = [Res("sqb%d" % i) for i in range(NSQ)]
    sq_rr = [0]

    def getsq():
        i = sq_rr[0]
        sq_rr[0] = (i + 1) % NSQ
        return sqb[i], r_sqb[i]

    cA = Carver(0)
    wk_sb = cA.get([128, 8, 704], BF16)
    wkvup_sb = cA.get([128, 2, 1024], BF16)
    r_wk, r_wkvup = Res("wk"), Res("wkvup")
    off_after_w = cA.off
    tBk = [cA.get([128, 2, TB], F32) for i in range(2)]
    ka_st = [cA.get([128, TB], BF16) for i in range(2)]
    kb_st = [cA.get([128, 4, TB], BF16) for i in range(2)]
    kr_st = [cA.get([128, TB], BF16) for i in range(2)]
    va_st = [cA.get([128, 2, 4, 64], BF16) for i in range(2)]
    vb_st = [cA.get([128, 8, 4, 64], BF16) for i in range(2)]
    kvn = [cA.get([128, 2, TB], BF16) for i in range(2)]
    r_tBk = [Res("tBk%d" % i) for i in range(2)]
    r_ka_st = [Res("ka_st%d" % i) for i in range(2)]
    r_kb_st = [Res("kb_st%d" % i) for i in range(2)]
    r_kr_st = [Res("kr_st%d" % i) for i in range(2)]
    r_va_st = [Res("va_st%d" % i) for i in range(2)]
    r_vb_st = [Res("vb_st%d" % i) for i in range(2)]
    r_kvn = [Res("kvn%d" % i) for i in range(2)]
    resA = r_tBk + r_ka_st + r_kb_st + r_kr_st + r_va_st + r_vb_st + r_kvn

    c0 = Carver(off_after_w)
    consts_f = c0.get([128, 3, 128], F32)
    cS = Carver(cA.off)
    stgs = [cS.get([128, 8, 512], F32) for _ in range(2)]
    r_stgs = [Res("stg0"), Res("stg1")]
    stg_rr = [0]
    rows = c0.get([128, 2048], F32)
    gprow = c0.get([128, 2, 1024], F32)
    onesrow = c0.get([128, 128], F32)
    r_rows, r_onesrow, r_cf, r_gprow = Res("rows"), Res("onesrow"), Res("cf"), Res("gprow")
    res0 = [r_rows, r_onesrow, r_cf, r_gprow]

    cB = Carver(0)
    tB = cB.get([128, 2, TB], F32)
    r_tB = Res("tB")
    qaT = cB.get([128, 8, TB], BF16)
    r_qaT = [Res("qaT%d" % i) for i in range(8)]
    qbT = cB.get([128, 8, TB], BF16)
    r_qbT = [Res("qbT%d" % i) for i in range(8)]
    agT = cB.get([128, 4, TB], BF16)
    r_agT = [Res("agT%d" % i) for i in range(4)]
    bgT = cB.get([128, 4, TB], BF16)
    r_bgT = [Res("bgT%d" % i) for i in range(4)]
    bqn = cB.get([128, 3, TB], BF16)
    r_bqn = [Res("bqn%d" % i) for i in range(3)]
    yaT = cB.get([128, 4, TB], BF16)
    ybT = cB.get([128, 4, TB], BF16)
    r_yaT = [[Res("yaT%d_%d" % (i, k)) for k in range(2)] for i in range(4)]
    r_ybT = [[Res("ybT%d_%d" % (i, k)) for k in range(2)] for i in range(4)]
    mergedT = cB.get([128, 8, TB], BF16)
    r_merged = [Res("merged%d" % i) for i in range(8)]
    NKB = 4
    kbuf = [cB.get([128, CK], BF16) for i in range(NKB)]
    vbuf = [cB.get([128, TPC, 192], BF16) for i in range(NKB)]
    r_kbuf = [Res("kbuf%d" % i) for i in range(NKB)]
    r_vbuf = [Res("vbuf%d" % i) for i in range(NKB)]
    kv_rr = [0]
    NPT = 3
    PT = [cB.get([128, 2 * TB], BF16) for i in range(NPT)]
    r_PT = [Res("PT%d" % i) for i in range(NPT)]
    pt_rr = [0]
    sg_rr = [0]
    o_rr = [0]
    rcs = [cB.get([128, TB], F32) for i in range(2)]
    r_rcs = [Res("rcs%d" % i) for i in range(2)]
    rc_rr = [0]
    smt = [cB.get([128, TB], BF16) for i in range(2)]
    r_smt = [Res("smt%d" % i) for i in range(2)]
    sm_rr = [0]
    ytile = cB.get([128, D], F32)
    r_ytile = Res("ytile")
    resB = ([r_tB] + r_qaT + r_qbT + r_agT + r_bgT + r_bqn + [x for l in r_yaT for x in l] + [x for l in r_ybT for x in l]
            + r_merged + r_kbuf + r_vbuf + r_PT + r_rcs + r_smt + [r_ytile])

    P.op("sp", "dma_start", dict(out=consts_f, in_=consts_d[:, :, :]), writes=[r_cf], dma_key="consts")
    P.op("sp", "dma_start", dict(out=vecs[:], in_=vecs_d[:, :]), writes=[r_vecs], dma_key="vecs")
    P.op("sp", "dma_start", dict(out=rows[0:1, :], in_=rows_d[:, :]), writes=[r_rows], dma_key="rows")
    P.op("dve", "tensor_copy", dict(out=consts_b[:], in_=consts_f), reads=[r_cf], writes=[r_consts])
    P.op("dve", "memset", dict(ap=onesrow[0:1, :], constant=1.0), writes=[r_onesrow])
    P.op("act", "activation", dict(out=scT[:].rearrange("p k s -> p (k s)"), in_=vecs[:, V_CT:V_CT + 16], func=AF.Silu),
         reads=[r_vecs], writes=[r_mod])

    adaw_v = ada_w.rearrange("(k p) n -> p k n", p=128)
    modb = bank()
    for g in range(6):
        sgi = stg_rr[0]
        stg_rr[0] = 1 - sgi
        stg, r_stg = stgs[sgi], r_stgs[sgi]
        P.op("sp", "dma_start", dict(out=stg, in_=adaw_v[:, :, g * 512:(g + 1) * 512]), writes=[r_stg], dma_key="stg%d" % sgi)
        if g < 4:
            for b4 in range(4):
                blk = g * 4 + b4
                for k in range(8):
                    P.op("pe", "matmul", dict(out=ps[:, modb, blk * 2:blk * 2 + 2], lhsT=stg[:, k, b4 * 128:(b4 + 1) * 128], rhs=scT[:, k, :],
                                              start=(k == 0), stop=(k == 7)), reads=[r_stg, r_mod], writes=[psr[modb]])
        else:
            g2 = g - 4
            for s in range(2):
                gb = bank()
                for k in range(8):
                    P.op("pe", "matmul", dict(out=ps[0:1, gb, :], lhsT=scT[:, k, s:s + 1], rhs=stg[:, k, :], start=(k == 0), stop=(k == 7)),
                         reads=[r_stg, r_mod], writes=[psr[gb]])
                P.op("dve", "tensor_tensor", dict(out=gprow[0:1, s, g2 * 512:(g2 + 1) * 512], in0=ps[0:1, gb, :],
                                                  in1=rows[0:1, g2 * 512:(g2 + 1) * 512], op=ALU.add),
                     reads=[psr[gb], r_rows], writes=[r_gprow])
                P.op("dve", "tensor_tensor", dict(out=gprow[0:1, s, g2 * 512:(g2 + 1) * 512], in0=gprow[0:1, s, g2 * 512:(g2 + 1) * 512],
                                                  in1=rows[0:1, 1024 + g2 * 512:1024 + (g2 + 1) * 512], op=ALU.mult),
                     reads=[r_gprow, r_rows], writes=[r_gprow])
    modv = ps[:, modb, 0:32].rearrange("p (b s) -> p s b", s=2)
    for s in range(2):
        P.op("dve", "tensor_tensor", dict(out=shT[:, s, :], in0=modv[:, s, 0:8], in1=vecs[:, V_ADAB:V_ADAB + 8], op=ALU.add),
             reads=[psr[modb], r_vecs], writes=[r_mod])
        P.op("dve", "tensor_tensor", dict(out=aT[:, s, :], in0=modv[:, s, 8:16], in1=vecs[:, V_ADAB + 8:V_ADAB + 16], op=ALU.add),
             reads=[psr[modb], r_vecs], writes=[r_mod])
        P.op("dve", "scalar_tensor_tensor", dict(out=aT[:, s, :], in0=aT[:, s, :], scalar=1.0, in1=vecs[:, V_PREG:V_PREG + 8],
                                                 op0=ALU.add, op1=ALU.mult), reads=[r_mod, r_vecs], writes=[r_mod])
    for s in range(2):
        for hf in range(2):
            gb = bank()
            P.op("pe", "matmul", dict(out=ps[:, gb, :], lhsT=onesrow[0:1, :], rhs=gprow[0:1, s, hf * 512:(hf + 1) * 512], start=True, stop=True),
                 reads=[r_gprow, r_onesrow], writes=[psr[gb]])
            P.op("dve", "tensor_copy", dict(out=gp_bc[:, s, hf * 512:(hf + 1) * 512], in_=ps[:, gb, :]), reads=[psr[gb]], writes=[r_gp])

    win_v = w_in.rearrange("(k p) n -> p k n", p=128)
    cv_rr = [0]

    def convert(loads, width, dst_ap, dst_res, kch=8):
        sgi = stg_rr[0]
        stg_rr[0] = 1 - sgi
        stg, r_stg = stgs[sgi], r_stgs[sgi]
        P.op("sp", "dma_start", [dict(out=stg[:, 0:kch, dc:dc + w], in_=src) for (dc, src, w) in loads], writes=[r_stg], dma_key="stg%d" % sgi)
        if cv_rr[0] % 2 == 0:
            P.op("dve", "tensor_copy", dict(out=dst_ap, in_=stg[:, 0:kch, 0:width]), reads=[r_stg], writes=[dst_res])
        else:
            P.op("act", "activation", dict(out=dst_ap, in_=stg[:, 0:kch, 0:width], func=AF.Copy), reads=[r_stg], writes=[dst_res])
        cv_rr[0] += 1

    def cols(c0_, w):
        return win_v[:, :, c0_:c0_ + w]

    def perm_loads(dst0, src_v, src_c0, nheads, dh):
        hf = dh // 2
        lds = []
        for h in range(nheads):
            lds.append((dst0 + h * dh, src_v[:, :, src_c0 + h * dh + hf:src_c0 + h * dh + dh], hf))
            lds.append((dst0 + h * dh + hf, src_v[:, :, src_c0 + h * dh:src_c0 + h * dh + hf], hf))
        return lds

    convert([(0, cols(C_AK, 128), 128)] + perm_loads(128, win_v, C_AK, 2, 64) + [(256, cols(C_BKV, 256), 256)], 512, wk_sb[:, :, 0:512], r_wk)
    convert([(0, cols(C_BKR, 32), 32)] + perm_loads(32, win_v, C_BKR, 1, 32) + [(64, cols(C_AV, 128), 128)], 192, wk_sb[:, :, 512:704], r_wk)

    late_jobs = []

    def wq_group(g, loads, width, kch=8):
        i = wb_rr[0]
        wb_rr[0] = (i + 1) % NWB
        wv = wbuf[i][:, 0:kch * width].rearrange("p (k n) -> p k n", n=width)
        convert(loads, width, wv, r_wbuf[i], kch=kch)
        P.op("pool", "dma_start", dict(out=wst[g, :, 0:kch * width], in_=wbuf[i][:, 0:kch * width]),
             reads=[r_wbuf[i]], writes=[r_wst[g]], dma_key="wbufS%d" % i)

    for g in range(2):
        lds = []
        for pi in range(2):
            pr = g * 2 + pi
            lds.append((pi * 256, cols(C_AQ + pr * 128, 128), 128))
            lds += perm_loads(pi * 256 + 128, win_v, C_AQ + pr * 128, 2, 64)
        late_jobs.append(partial(wq_group, g, lds, 512))
    late_jobs.append(partial(wq_group, 2, [(0, cols(C_AG, 512), 512)], 512))
    late_jobs.append(partial(wq_group, 3, [(0, cols(C_BQ, 384), 384)], 384))
    late_jobs.append(partial(wq_group, 4, [(0, cols(C_BG, 512), 512)], 512))
    late_jobs.append(partial(wq_group, 5, [(0, cols(C_MA, 512), 512)], 512))
    late_jobs.append(partial(wq_group, 6, [(0, cols(C_MA + 512, 512), 512)], 512))
    late_jobs.append(partial(wq_group, 7, [(0, cols(C_MB, 512), 512)], 512))
    late_jobs.append(partial(wq_group, 8, [(0, cols(C_MB + 512, 512), 512)], 512))
    wo_v = w_o.rearrange("(k p) n -> p k n", p=128)
    late_jobs.append(partial(wq_group, 9, [(0, wo_v[:, :, 0:512], 512)], 512))
    late_jobs.append(partial(wq_group, 10, [(0, wo_v[:, :, 512:1024], 512)], 512))
    def conv_ab(g, wv_):
        i = wb_rr[0]
        wb_rr[0] = (i + 1) % NWB
        wv = wbuf[i][:, :].rearrange("p (k n) -> p k n", n=1024)
        for hf in range(2):
            convert([(0, wv_[:, :, hf * 512:(hf + 1) * 512], 512)], 512, wv[:, :, hf * 512:(hf + 1) * 512], r_wbuf[i], kch=4)
        P.op("pool", "dma_start", dict(out=wst[g, :, :], in_=wbuf[i][:, :]), reads=[r_wbuf[i]], writes=[r_wst[g]], dma_key="wbufS%d" % i)

    late_jobs.append(partial(conv_ab, 11, a_out.rearrange("(k p) n -> p k n", p=128)))
    late_jobs.append(partial(conv_ab, 12, b_out.rearrange("(k p) n -> p k n", p=128)))
    kvu_v = b_kv_up.rearrange("(k p) (h t d) -> p k t h d", p=128, t=2, d=64)
    for t in range(2):
        sgi = stg_rr[0]
        stg_rr[0] = 1 - sgi
        stg, r_stg = stgs[sgi], r_stgs[sgi]
        P.op("sp", "dma_start", [dict(out=stg[:, k, :].rearrange("p (h d) -> p h d", d=64), in_=kvu_v[:, k, t, :, :]) for k in range(2)],
             writes=[r_stg], dma_key="stg%d" % sgi)
        P.op("dve", "tensor_copy", dict(out=wkvup_sb[:, :, t * 512:(t + 1) * 512], in_=stg[:, 0:2, :]), reads=[r_stg], writes=[r_wkvup])
    qu_v = b_q_up.rearrange("(k p) n -> p k n", p=128)
    late_jobs.append(partial(convert, [(0, qu_v[:, :, 0:512], 512)], 512, wqup_sb[:, :, 0:512], r_wqup, 3))
    late_jobs.append(partial(convert, [(0, qu_v[:, :, 512:768], 256)], 256, wqup_sb[:, :, 512:768], r_wqup, 3))
    for part in range(2):
        lds = []
        for hh in range(4):
            h = part * 4 + hh
            lds.append((hh * 96, qu_v[:, :, h * 96:h * 96 + 64], 64))
            lds.append((hh * 96 + 64, qu_v[:, :, h * 96 + 80:h * 96 + 96], 16))
            lds.append((hh * 96 + 80, qu_v[:, :, h * 96 + 64:h * 96 + 80], 16))
        late_jobs.append(partial(convert, lds, 384, wqup_sb[:, :, 768 + part * 384:768 + (part + 1) * 384], r_wqup, 3))

    P.op("pool", "nop", dict(), writes=res0 + resA)

    def rsqrt_ops(out_ap, in_ap, scale, in_res, out_res):
        if RSQRT_LN:
            P.op("act", "activation", dict(out=out_ap, in_=in_ap, func=AF.Ln, scale=scale, bias=EPS), reads=in_res, writes=[out_res])
            P.op("act", "activation", dict(out=out_ap, in_=out_ap, func=AF.Exp, scale=-0.5), reads=[out_res], writes=[out_res])
        else:
            P.op("act", "activation", dict(out=out_ap, in_=in_ap, func=AF.Sqrt, scale=scale, bias=EPS), reads=in_res, writes=[out_res])
            P.op("dve", "reciprocal", dict(out=out_ap, in_=out_ap), reads=[out_res], writes=[out_res])

    def front_end(s, tok0):
        hi = hT_rr[0]
        hT_rr[0] = (hi + 1) % 2
        tb4 = [bank() for _ in range(4)]
        for j in range(4):
            xi = xt_rr[0]
            xt_rr[0] = (xi + 1) % NXT
            P.op("sp", "dma_start", dict(out=xt[xi][:], in_=xk[s][tok0 + j * 128:tok0 + (j + 1) * 128, :]), writes=[r_xt[xi]], dma_key="xt%d" % xi)
            si = ss_rr[0]
            ss_rr[0] = (si + 1) % 8
            P.op("act", "activation", dict(out=junk[:], in_=xt[xi][:], func=AF.Square, accum_out=ss[:, si:si + 1]),
                 reads=[r_xt[xi]], writes=[r_ss[si]])
            rsqrt_ops(ss[:, si:si + 1], ss[:, si:si + 1], 1.0 / D, [r_ss[si]], r_ss[si])
            ni = xn_rr[0]
            xn_rr[0] = (ni + 1) % 2
            P.op("dve", "tensor_scalar", dict(out=xn[ni][:], in0=xt[xi][:], scalar1=ss[:, si:si + 1], scalar2=None, op0=ALU.mult),
                 reads=[r_xt[xi], r_ss[si]], writes=[r_xn[ni]])
            for c in range(8):
                b = tb4[c // 2]
                off = (c % 2) * 512 + j * 128
                P.op("pe", "transpose", dict(out=ps[:, b, :].bitcast(BF16)[:, off:off + 128], in_=xn[ni][:, c * 128:(c + 1) * 128], identity=ident),
                     reads=[r_xn[ni], r_consts], writes=[psr[b]])
        for c in range(8):
            b = tb4[c // 2]
            off = (c % 2) * 512
            P.op("dve", "tensor_scalar", dict(out=hT[hi][:, c, :], in0=ps[:, b, :].bitcast(BF16)[:, off:off + 512], scalar1=aT[:, s, c:c + 1],
                                              scalar2=shT[:, s, c:c + 1], op0=ALU.mult, op1=ALU.add),
                 reads=[psr[b], r_mod], writes=[r_hT[hi][c]])
        return hi

    def proj8(hi, wv, c0_, M, w_res):
        b = bank()
        for c in range(8):
            P.op("pe", "matmul", dict(out=ps[0:M, b, :], lhsT=wv[:, c, c0_:c0_ + M], rhs=hT[hi][:, c, :], start=(c == 0), stop=(c == 7)),
                 reads=[w_res, r_hT[hi][c]], writes=[psr[b]])
        return b

    def load_tables(s, tok0, kind):
        ti = tab_rr[0]
        tab_rr[0] = (ti + 1) % 2
        src = tabA[s][:, :, tok0:tok0 + TB].rearrange("t p n -> p t n")
        P.op("sp", "dma_start", [dict(out=tA[ti][0:64, :, :], in_=src), dict(out=tA[ti][64:128, :, :], in_=src)],
             writes=[r_tA[ti]], dma_key="tA%d" % ti)
        gc, gr = (V_GAK, V_GAKR) if kind == "k" else (V_GAQ, V_GAQR)
        P.op("dve", "tensor_scalar", dict(out=tA[ti][:, 0, :], in0=tA[ti][:, 0, :], scalar1=vecs[:, gc:gc + 1], scalar2=None, op0=ALU.mult),
             reads=[r_tA[ti], r_vecs], writes=[r_tA[ti]])
        P.op("dve", "tensor_scalar", dict(out=tA[ti][:, 1, :], in0=tA[ti][:, 1, :], scalar1=vecs[:, gr:gr + 1], scalar2=None, op0=ALU.mult),
             reads=[r_tA[ti], r_vecs], writes=[r_tA[ti]])
        srcB = tabB[s][:, :, tok0:tok0 + TB].rearrange("t p n -> p t n")
        if kind == "k":
            P.op("sp", "dma_start", dict(out=tBk[ti][0:32, :, :], in_=srcB), writes=[r_tBk[ti]], dma_key="tBk%d" % ti)
        else:
            P.op("sp", "dma_start", dict(out=tB[64:96, :, :], in_=srcB), writes=[r_tB], dma_key="tB")
        return ti

    def rstd_from(msb, Dn):
        t, rt = gettmp()
        rsqrt_ops(t[:], ps[:, msb, :], 1.0 / Dn, [psr[msb]], rt)
        return t, rt

    def hnr_square(bx):
        sq, rsq = getsq()
        P.op("act", "activation", dict(out=sq[:], in_=ps[:, bx, :], func=AF.Square), reads=[psr[bx]], writes=[rsq])
        return sq, rsq

    def hnr_mix(bx, by, ti, after=()):
        t1, r1 = gettmp()
        t2, r2 = gettmp()
        P.op("dve", "tensor_tensor", dict(out=t1[:], in0=ps[:, bx, :], in1=tA[ti][:, 0, :], op=ALU.mult), reads=[psr[bx], r_tA[ti]] + list(after), writes=[r1])
        P.op("dve", "tensor_tensor", dict(out=t2[:], in0=ps[:, by, :], in1=tA[ti][:, 1, :], op=ALU.mult), reads=[psr[by], r_tA[ti]], writes=[r2])
        P.op("dve", "tensor_tensor", dict(out=t1[:], in0=t1[:], in1=t2[:], op=ALU.add), reads=[r1, r2], writes=[r1])
        return t1, r1

    def hnr_finish(sqp, mix, out_ap, out_res):
        sq, rsq = sqp
        t1, r1 = mix
        mb = bank()
        P.op("pe", "matmul", dict(out=ps[:, mb, :], lhsT=bones_b, rhs=sq[:], start=True, stop=True), reads=[rsq, r_consts], writes=[psr[mb]])
        rt, rrt = rstd_from(mb, 64)
        if isinstance(out_ap, list):
            for (oap, ores, psl) in out_ap:
                P.op("dve", "tensor_tensor", dict(out=oap, in0=t1[psl, :], in1=rt[psl, :], op=ALU.mult), reads=[r1, rrt], writes=[ores])
        else:
            P.op("dve", "tensor_tensor", dict(out=out_ap, in0=t1[:], in1=rt[:], op=ALU.mult), reads=[r1, rrt], writes=[out_res])

    def head_norm_rope(bx, by, ti, out_ap, out_res):
        sqp = hnr_square(bx)
        mix = hnr_mix(bx, by, ti, after=[sqp[1]])
        hnr_finish(sqp, mix, out_ap, out_res)

    def ln_square(banks):
        sqs = []
        for c, b in enumerate(banks):
            sq, rsq = getsq()
            P.op("act", "activation", dict(out=sq[:], in_=ps[:, b, :], func=AF.Square), reads=[psr[b]], writes=[rsq])
            sqs.append((sq, rsq))
        return sqs

    def ln_finish(banks, sqs, Dn, gcol, out_tile, out_res):
        mb = bank()
        for c, (sq, rsq) in enumerate(sqs):
            P.op("pe", "matmul", dict(out=ps[:, mb, :], lhsT=ones_b, rhs=sq[:], start=(c == 0), stop=(c == len(sqs) - 1)),
                 reads=[rsq, r_consts], writes=[psr[mb]])
        rt, rrt = rstd_from(mb, Dn)
        for c, b in enumerate(banks):
            P.op("dve", "scalar_tensor_tensor", dict(out=out_tile[:, c, :], in0=ps[:, b, :], scalar=vecs[:, gcol + c:gcol + c + 1], in1=rt[:],
                                                     op0=ALU.mult, op1=ALU.mult), reads=[psr[b], r_vecs, rrt], writes=[out_res[c]])

    def latent_norm(banks, Dn, gcol, out_tile, out_res):
        ln_finish(banks, ln_square(banks), Dn, gcol, out_tile, out_res)

    blocksA = [(s, tbk) for s in range(2) if s in [g[0] for g in segs] for tbk in range(seqs[s]["n"] // TB)]
    fe_done = {}

    def fe_A(bi):
        s_, tbk_ = blocksA[bi]
        hi_ = front_end(s_, tbk_ * TB)
        fe_done[bi] = (load_tables(s_, tbk_ * TB, "k"), hi_)

    for b_ in range(min(2, len(blocksA))):
        fe_A(b_)
    for bi, (s, tbk) in enumerate(blocksA):
        sq_ = seqs[s]
        if True:
            tok0 = tbk * TB
            pi = bi % 2
            ti, hi = fe_done.pop(bi)
            if late_jobs:
                late_jobs.pop(0)()
            bx = proj8(hi, wk_sb, 0, 128, r_wk)
            by = proj8(hi, wk_sb, 128, 128, r_wk)
            b0 = proj8(hi, wk_sb, 256, 128, r_wk)
            b1 = proj8(hi, wk_sb, 384, 128, r_wk)
            sqp = hnr_square(bx)
            sqs = ln_square([b0, b1])
            mix = hnr_mix(bx, by, ti, after=[sqp[1]])
            brx = proj8(hi, wk_sb, 512, 32, r_wk)
            bry = proj8(hi, wk_sb, 544, 32, r_wk)
            va_ = bank()
            for j in range(4):
                for c in range(8):
                    P.op("pe", "matmul", dict(out=ps[:, va_, j * 128:(j + 1) * 128], lhsT=hT[hi][:, c, j * 128:(j + 1) * 128], rhs=wk_sb[:, c, 576:704],
                                              start=(c == 0), stop=(c == 7)), reads=[r_wk, r_hT[hi][c]], writes=[psr[va_]])
            ch, sl = (tbk * 4) // TPC, (tbk * 4) % TPC
            ln_finish([b0, b1], sqs, 256, V_BKVG, kvn[pi], [r_kvn[pi], r_kvn[pi]])
            hnr_finish(sqp, mix, ka_st[pi][:], r_ka_st[pi])
            P.op("pool", "dma_start", dict(out=sq_["KA"][:, tok0:tok0 + TB], in_=ka_st[pi][:]), reads=[r_ka_st[pi]], dma_key="ka_st%d" % pi)
            t1, r1 = gettmp()
            t2, r2 = gettmp()
            P.op("dve", "tensor_tensor", dict(out=t1[0:32, :], in0=ps[0:32, brx, :], in1=tBk[ti][0:32, 0, :], op=ALU.mult),
                 reads=[psr[brx], r_tBk[ti]], writes=[r1])
            P.op("dve", "tensor_tensor", dict(out=t2[0:32, :], in0=ps[0:32, bry, :], in1=tBk[ti][0:32, 1, :], op=ALU.mult),
                 reads=[psr[bry], r_tBk[ti]], writes=[r2])
            P.op("dve", "tensor_tensor", dict(out=kr_st[pi][0:32, :], in0=t1[0:32, :], in1=t2[0:32, :], op=ALU.add),
                 reads=[r1, r2], writes=[r_kr_st[pi]])
            P.op("pool", "dma_start", dict(out=sq_["KR"][:, tok0:tok0 + TB], in_=kr_st[pi][0:32, :]), reads=[r_kr_st[pi]], dma_key="kr_st%d" % pi)
            P.op("dve", "tensor_copy", dict(out=va_st[pi][:].rearrange("p g j d -> p j g d"),
                                            in_=ps[:, va_, :].rearrange("p (j g d) -> p j g d", g=2, d=64)), reads=[psr[va_]], writes=[r_va_st[pi]])
            P.op("pool", "dma_start", dict(out=sq_["VA"][:, ch, :, sl:sl + 4, :].rearrange("g p j d -> p g (j d)"),
                                           in_=va_st[pi][:].rearrange("p g j d -> p g (j d)")), reads=[r_va_st[pi]], dma_key="va_st%d" % pi)
            if bi + 2 < len(blocksA):
                fe_A(bi + 2)
            for pr in range(4):
                kb_ = bank()
                for c in range(2):
                    P.op("pe", "matmul", dict(out=ps[:, kb_, :], lhsT=wkvup_sb[:, c, pr * 128:(pr + 1) * 128], rhs=kvn[pi][:, c, :],
                                              start=(c == 0), stop=(c == 1)), reads=[r_wkvup, r_kvn[pi]], writes=[psr[kb_]])
                P.op("act", "activation", dict(out=kb_st[pi][:, pr, :], in_=ps[:, kb_, :], func=AF.Copy), reads=[psr[kb_]], writes=[r_kb_st[pi]])
            P.op("pool", "dma_start", dict(out=sq_["KBn"][:, :, tok0:tok0 + TB].rearrange("r p n -> p r n"), in_=kb_st[pi][:]),
                 reads=[r_kb_st[pi]], dma_key="kb_st%d" % pi)
            for j in range(4):
                vb_ = bank()
                for c in range(2):
                    P.op("pe", "matmul", dict(out=ps[:, vb_, :], lhsT=kvn[pi][:, c, j * 128:(j + 1) * 128], rhs=wkvup_sb[:, c, 512:1024],
                                              start=(c == 0), stop=(c == 1)), reads=[r_wkvup, r_kvn[pi]], writes=[psr[vb_]])
                P.op("dve", "tensor_copy", dict(out=vb_st[pi][:, :, j, :], in_=ps[:, vb_, :].rearrange("p (h d) -> p h d", d=64)),
                     reads=[psr[vb_]], writes=[r_vb_st[pi]])
            P.op("pool", "dma_start", dict(out=sq_["VB"][:, ch, :, sl:sl + 4, :].rearrange("h p j d -> p h (j d)"),
                                           in_=vb_st[pi][:].rearrange("p h j d -> p h (j d)")), reads=[r_vb_st[pi]], dma_key="vb_st%d" % pi)

    while late_jobs:
        late_jobs.pop(0)()
    r_scr = Res("scratch")
    P.op("pool", "nop", dict(), writes=resA + [r_wk, r_wkvup] + r_stgs + resB + [r_scr])
    for i in range(NKB):
        P.op("pool", "memset", dict(ap=vbuf[i][:], constant=1.0), writes=[r_vbuf[i]])
    for h8 in range(8):
        P.op("pool", "memset", dict(ap=qaT[:, h8, :], constant=0.0), writes=[r_qaT[h8]])
    P.op("pool", "memset", dict(ap=tB[0:64, 0, :], constant=1.0), writes=[r_tB])
    P.op("pool", "memset", dict(ap=tB[0:64, 1, :], constant=0.0), writes=[r_tB])

    def wload(g, nel=4096):
        i = wb_rr[0]
        wb_rr[0] = (i + 1) % NWB
        P.op("sp", "dma_start", dict(out=wbuf[i][:, 0:nel], in_=wst[g, :, 0:nel]), reads=[r_wst[g]], writes=[r_wbuf[i]], dma_key="wbuf%d" % i)
        return i

    def wview(i, width=512):
        return wbuf[i][:, 0:8 * width].rearrange("p (k n) -> p k n", n=width)

    feB = {}
    for (s, qlo, qhi) in segs:
        sq_ = seqs[s]
        n, nch = sq_["n"], sq_["nch"]
        for qb in range(qlo, sq_["q"] // TB if qhi is None else qhi):
            tok0 = qb * TB
            if (s, qb) in feB:
                ti, hi = feB.pop((s, qb))
            else:
                ti, hi = load_tables(s, tok0, "q"), front_end(s, tok0)
            for g in range(2):
                wi = wload(g)
                for pi2 in range(2):
                    pr = g * 2 + pi2
                    bx = proj8(hi, wview(wi), pi2 * 256, 128, r_wbuf[wi])
                    by = proj8(hi, wview(wi), pi2 * 256 + 128, 128, r_wbuf[wi])
                    head_norm_rope(bx, by, ti, [(qaT[0:64, 2 * pr, :], r_qaT[2 * pr], slice(0, 64)),
                                                (qaT[64:128, 2 * pr + 1, :], r_qaT[2 * pr + 1], slice(64, 128))], None)
            wi = wload(2)
            for i4 in range(4):
                b = proj8(hi, wview(wi), i4 * 128, 128, r_wbuf[wi])
                P.op("act", "activation", dict(out=agT[:, i4, :], in_=ps[:, b, :], func=AF.Silu), reads=[psr[b]], writes=[r_agT[i4]])
            wi = wload(3, 8 * 384)
            bqb = [proj8(hi, wview(wi, 384), i3 * 128, 128, r_wbuf[wi]) for i3 in range(3)]
            latent_norm(bqb, 384, V_BQG, bqn, r_bqn)
            wi = wload(4)
            for i4 in range(4):
                b = proj8(hi, wview(wi), i4 * 128, 128, r_wbuf[wi])
                P.op("act", "activation", dict(out=bgT[:, i4, :], in_=ps[:, b, :], func=AF.Silu), reads=[psr[b]], writes=[r_bgT[i4]])
            for h in range(8):
                bx, by = bank(), bank()
                for (bb, off) in ((bx, 0), (by, 768)):
                    for c in range(3):
                        P.op("pe", "matmul", dict(out=ps[0:96, bb, :], lhsT=wqup_sb[:, c, off + h * 96:off + (h + 1) * 96], rhs=bqn[:, c, :],
                                                  start=(c == 0), stop=(c == 2)), reads=[r_wqup, r_bqn[c]], writes=[psr[bb]])
                t1, r1 = gettmp()
                t2, r2 = gettmp()
                P.op("dve", "tensor_tensor", dict(out=t1[0:96, :], in0=ps[0:96, bx, :], in1=tB[0:96, 0, :], op=ALU.mult),
                     reads=[psr[bx], r_tB], writes=[r1])
                P.op("dve", "tensor_tensor", dict(out=t2[0:96, :], in0=ps[0:96, by, :], in1=tB[0:96, 1, :], op=ALU.mult),
                     reads=[psr[by], r_tB], writes=[r2])
                P.op("dve", "tensor_tensor", dict(out=qbT[0:96, h, :], in0=t1[0:96, :], in1=t2[0:96, :], op=ALU.add),
                     reads=[r1, r2], writes=[r_qbT[h]])

            qlist = [(s2, q2) for (s2, lo2, hi2) in segs for q2 in range(lo2, seqs[s2]["q"] // TB if hi2 is None else hi2)]
            qpos = qlist.index((s, qb))
            if HOIST_FE and qpos + 1 < len(qlist):
                s2, q2 = qlist[qpos + 1]
                feB[(s2, q2)] = (load_tables(s2, q2 * TB, "q"), front_end(s2, q2 * TB))

            def kv_load(kind, idx, chk, sq_=sq_):
                i = kv_rr[0]
                kv_rr[0] = (i + 1) % NKB
                k0 = chk * CK
                if kind == "A":
                    src = sq_["KA"][idx * 64:(idx + 1) * 64, k0:k0 + CK]
                    P.op("sp", "dma_start", [dict(out=kbuf[i][0:64, :], in_=src), dict(out=kbuf[i][64:128, :], in_=src)],
                         reads=[r_scr], writes=[r_kbuf[i]], dma_key="kbuf%d" % i)
                    P.op("sp", "dma_start", dict(out=vbuf[i][:, :, 64:128], in_=sq_["VA"][idx, chk, :, :, :]),
                         reads=[r_scr], writes=[r_vbuf[i]], dma_key="vbuf%d" % i)
                else:
                    h = idx
                    P.op("sp", "dma_start", [dict(out=kbuf[i][0:64, :], in_=sq_["KBn"][h // 2, (h % 2) * 64:(h % 2) * 64 + 64, k0:k0 + CK]),
                                             dict(out=kbuf[i][64:96, :], in_=sq_["KR"][:, k0:k0 + CK])],
                         reads=[r_scr], writes=[r_kbuf[i]], dma_key="kbuf%d" % i)
                    P.op("sp", "dma_start", dict(out=vbuf[i][:, :, 64:128], in_=sq_["VB"][h, chk, :, :, :]),
                         reads=[r_scr], writes=[r_vbuf[i]], dma_key="vbuf%d" % i)
                return i

            units = [("A", g, [4 * g + k for k in range(4)]) for g in range(2)] + [("B", h, [h]) for h in range(8)]
            loads = [(u, chk) for u in range(len(units)) for chk in range(nch)]
            loaded = {}
            PF = 2
            nxt = 0
            pendq = []
            PVLAG = 2

            def emit_pv(st):
                for t in range(2):
                    P.op("pe", "matmul", dict(out=ps[:, st["ob"], :], lhsT=st["v"][t], rhs=PT[st["pt"]][:, t * TB:(t + 1) * TB],
                                              start=(st["first"] and t == 0), stop=(st["last"] and t == 1)),
                         reads=[r_vbuf[st["kvi"]], r_PT[st["pt"]]], writes=[psr[st["ob"]]])

            def finalize_head(kind, h, ob):
                par = h % 2
                nm = slice(64, 128) if par else slice(0, 64)
                dn = slice(0, 64) if par else slice(64, 128)
                ri = rc_rr[0]
                rc_rr[0] = (ri + 1) % 2
                P.op("dve", "reciprocal", dict(out=rcs[ri][dn, :], in_=ps[dn, ob, :]), reads=[psr[ob]], writes=[r_rcs[ri]])
                t1, r1 = gettmp()
                P.op("dve", "tensor_tensor", dict(out=t1[nm, :], in0=ps[nm, ob, :], in1=rcs[ri][dn, :], op=ALU.mult),
                     reads=[psr[ob], r_rcs[ri]], writes=[r1])
                gT, rg = (agT, r_agT) if kind == "A" else (bgT, r_bgT)
                yT, ry = (yaT, r_yaT) if kind == "A" else (ybT, r_ybT)
                P.op("pool", "tensor_tensor", dict(out=yT[nm, h // 2, :], in0=t1[nm, :], in1=gT[nm, h // 2, :], op=ALU.mult),
                     reads=[r1, rg[h // 2]], writes=[ry[h // 2][par]])

            li = 0
            for u, (kind, idx, heads) in enumerate(units):
                obs = {}
                for hh in heads:
                    obs[hh] = 4 + o_rr[0]
                    o_rr[0] = (o_rr[0] + 1) % 4
                scale = 0.125 if kind == "A" else 96 ** -0.5
                for chk in range(nch):
                    while nxt < len(loads) and nxt <= li + PF:
                        loaded[nxt] = kv_load(units[loads[nxt][0]][0], units[loads[nxt][0]][1], loads[nxt][1])
                        nxt += 1
                    kvi = loaded[li]
                    li += 1
                    for hh in heads:
                        par = hh % 2
                        if kind == "A":
                            ksl = slice(0, 128)
                            q_ap = qaT[:, hh, :]
                            q_res = r_qaT[hh]
                        else:
                            ksl = slice(0, 96)
                            q_ap = qbT[0:96, hh, :]
                            q_res = r_qbT[hh]
                        vcols = slice(0, 128) if par else slice(64, 192)
                        for gp in range(TPC // 2):
                            sg = sg_rr[0]
                            sg_rr[0] = (sg + 1) % 2
                            sb0 = sg * 2
                            pt = pt_rr[0]
                            pt_rr[0] = (pt + 1) % NPT
                            st = dict(ob=obs[hh], kvi=kvi, pt=pt, v=[vbuf[kvi][:, gp * 2 + t, vcols] for t in range(2)],
                                      first=(chk == 0 and gp == 0), last=(chk == nch - 1 and gp == TPC // 2 - 1))
                            for t in range(2):
                                kt = gp * 2 + t
                                P.op("pe", "matmul", dict(out=ps[:, sb0 + t, :], lhsT=kbuf[kvi][ksl, kt * 128:(kt + 1) * 128], rhs=q_ap,
                                                          start=True, stop=True), reads=[r_kbuf[kvi], q_res], writes=[psr[sb0 + t]])
                            P.op("act", "activation", dict(out=PT[pt][:], in_=ps[:, sb0:sb0 + 2, :].rearrange("p b n -> p (b n)"),
                                                           func=AF.Exp, scale=scale), reads=[psr[sb0], psr[sb0 + 1]], writes=[r_PT[pt]])
                            pendq.append((st, (kind, hh, obs[hh])))
                            if len(pendq) > PVLAG:
                                p0 = pendq.pop(0)
                                emit_pv(p0[0])
                                if p0[0]["last"]:
                                    finalize_head(*p0[1])
            while pendq:
                p0 = pendq.pop(0)
                emit_pv(p0[0])
                if p0[0]["last"]:
                    finalize_head(*p0[1])

            for (nm_, gw, g0, yT, ry) in (("a", 11, 5, yaT, r_yaT), ("b", 12, 7, ybT, r_ybT)):
                wo_ = wload(gw)
                wov = wbuf[wo_][:, :].rearrange("p (k n) -> p k n", n=1024)
                for cb in range(8):
                    if cb % 4 == 0:
                        wi = wload(g0 + cb // 4)
                    pa = bank()
                    for c in range(4):
                        P.op("pe", "matmul", dict(out=ps[:, pa, :], lhsT=wov[:, c, cb * 128:(cb + 1) * 128], rhs=yT[:, c, :],
                                                  start=(c == 0), stop=(c == 3)), reads=[r_wbuf[wo_], ry[c][0], ry[c][1]], writes=[psr[pa]])
                    pm = proj8(hi, wview(wi), (cb % 4) * 128, 128, r_wbuf[wi])
                    si = sm_rr[0]
                    sm_rr[0] = (si + 1) % 2
                    P.op("act", "activation", dict(out=smt[si][:], in_=ps[:, pm, :], func=AF.Sigmoid), reads=[psr[pm]], writes=[r_smt[si]])
                    if nm_ == "a":
                        P.op("dve", "tensor_tensor", dict(out=mergedT[:, cb, :], in0=ps[:, pa, :], in1=smt[si][:], op=ALU.mult),
                             reads=[psr[pa], r_smt[si]], writes=[r_merged[cb]])
                    else:
                        t1, r1 = gettmp()
                        P.op("dve", "tensor_tensor", dict(out=t1[:], in0=ps[:, pa, :], in1=smt[si][:], op=ALU.mult),
                             reads=[psr[pa], r_smt[si]], writes=[r1])
                        P.op("pool", "tensor_tensor", dict(out=mergedT[:, cb, :], in0=mergedT[:, cb, :], in1=t1[:], op=ALU.add),
                             reads=[r1, r_merged[cb]], writes=[r_merged[cb]])
            wo_i = [wload(9), wload(10)]
            for j in range(4):
                xi = xt_rr[0]
                xt_rr[0] = (xi + 1) % NXT
                P.op("sp", "dma_start", dict(out=xt[xi][:], in_=xk[s][tok0 + j * 128:tok0 + (j + 1) * 128, :]), writes=[r_xt[xi]], dma_key="xt%d" % xi)
                zb = [bank(), bank()]
                sis = []
                for hf in range(2):
                    for c in range(8):
                        P.op("pe", "matmul", dict(out=ps[:, zb[hf], :], lhsT=mergedT[:, c, j * 128:(j + 1) * 128], rhs=wview(wo_i[hf])[:, c, :],
                                                  start=(c == 0), stop=(c == 7)), reads=[r_merged[c], r_wbuf[wo_i[hf]]], writes=[psr[zb[hf]]])
                    si = st_rr[0]
                    st_rr[0] = (si + 1) % 8
                    sis.append(si)
                    P.op("act", "activation", dict(out=junk[:, 0:512], in_=ps[:, zb[hf], :], func=AF.Square, accum_out=st1[:, si:si + 1]),
                         reads=[psr[zb[hf]]], writes=[r_st1[si]])
                s0, s1 = sis
                P.op("dve", "tensor_tensor", dict(out=st1[:, s0:s0 + 1], in0=st1[:, s0:s0 + 1], in1=st1[:, s1:s1 + 1], op=ALU.add),
                     reads=[r_st1[s0], r_st1[s1]], writes=[r_st1[s0]])
                rsqrt_ops(st1[:, s0:s0 + 1], st1[:, s0:s0 + 1], 1.0 / D, [r_st1[s0]], r_st1[s0])
                for hf in range(2):
                    P.op("dve", "scalar_tensor_tensor", dict(out=ytile[:, hf * 512:(hf + 1) * 512], in0=ps[:, zb[hf], :], scalar=st1[:, s0:s0 + 1],
                                                             in1=gp_bc[:, s, hf * 512:(hf + 1) * 512], op0=ALU.mult, op1=ALU.mult),
                         reads=[psr[zb[hf]], r_st1[s0], r_gp], writes=[r_ytile])
                P.op("pool", "tensor_tensor", dict(out=ytile[:], in0=ytile[:], in1=xt[xi][:], op=ALU.add), reads=[r_ytile, r_xt[xi]], writes=[r_ytile])
                P.op("pool", "dma_start", dict(out=outs[s][tok0 + j * 128:tok0 + (j + 1) * 128, :], in_=ytile[:]), reads=[r_ytile], dma_key="ytile")

    P.finalize()
    with nc.Block() as block:
        @block.tensor
        def _(e):
            P.replay("pe", e)

        @block.scalar
        def _(e):
            P.replay("act", e)

        @block.vector
        def _(e):
            P.replay("dve", e)

        @block.gpsimd
        def _(e):
            P.replay("pool", e)
            P.final_wait(e)

        @block.sync
        def _(e):
            P.replay("sp", e)
    stack.close()
    return nc


def _rope_tab(n, d_rot):
    rows = n // GRID_W
    row_ids = np.repeat(np.arange(rows, dtype=np.float32), GRID_W)
    col_ids = np.tile(np.arange(GRID_W, dtype=np.float32), rows)
    d_axis = d_rot // 2
    inv = (ROPE_THETA ** (-np.arange(0, d_axis, 2, dtype=np.float32) / d_axis)).astype(np.float32)
    ang = np.concatenate([row_ids[:, None] * inv, col_ids[:, None] * inv], axis=-1).astype(np.float32)
    cos = np.cos(ang).astype(np.float32).T
    sin = np.sin(ang).astype(np.float32).T
    c = np.concatenate([cos, cos], axis=0)
    s = np.concatenate([-sin, sin], axis=0)
    return np.stack([c, s], axis=0)


_NC_CACHE = {}


def _run(NC, QP, QS, launches, x_prompt, x_sample, c_prompt, c_sample, ada_w, ada_b, pre_norm_g, post_norm_g, w_in,
         a_q_norm_g, a_k_norm_g, b_q_norm_g, b_q_up, b_kv_norm_g, b_kv_up, a_out, b_out, w_o):
    f = lambda a: np.ascontiguousarray(np.asarray(a, dtype=np.float32))
    NP, NS = NC * QP, 2 * QS
    ncs = []
    for segs in launches:
        key = (NC, QP, QS, segs)
        if key not in _NC_CACHE:
            _NC_CACHE[key] = _build(NC, QP, QS, segs)
        ncs.append(_NC_CACHE[key])
    xp = f(x_prompt)[0]
    xs = f(x_sample)
    tabA_p, tabB_p = _rope_tab(NP, 64), _rope_tab(NP, 32)
    tabA_s, tabB_s = _rope_tab(NS, 64), _rope_tab(NS, 32)
    consts = np.zeros((128, 3, 128), np.float32)
    consts[:, 0, :] = np.eye(128, dtype=np.float32)
    consts[:, 1, :] = 1.0
    consts[0:64, 2, 0:64] = 1.0
    consts[64:128, 2, 64:128] = 1.0
    ada_b0 = f(ada_b)[0]
    gq, gk = f(a_q_norm_g)[0], f(a_k_norm_g)[0]
    rot = lambda g: np.concatenate([g[32:], g[:32]])
    rows = np.concatenate([ada_b0[2048:3072], f(post_norm_g)[0]])[None, :]
    common = dict(rows=f(rows), consts=consts, ada_w=f(ada_w)[0], w_in=f(w_in)[0], b_q_up=f(b_q_up)[0], b_kv_up=f(b_kv_up)[0],
                  a_out=f(a_out)[0], b_out=f(b_out)[0], w_o=f(w_o)[0])
    in_maps = []
    for c in range(NC):
        si, hf = c // 2, c % 2
        vecs = np.zeros((128, 64), np.float32)
        vecs[:, 0:8] = f(pre_norm_g)[0].reshape(8, 128).T
        vecs[:, 8:32] = ada_b0.reshape(24, 128).T
        vecs[:, 32] = np.tile(gq, 2)
        vecs[:, 33] = np.tile(rot(gq), 2)
        vecs[:, 34] = np.tile(gk, 2)
        vecs[:, 35] = np.tile(rot(gk), 2)
        vecs[:, 36:39] = f(b_q_norm_g)[0].reshape(3, 128).T
        vecs[:, 39:41] = f(b_kv_norm_g)[0].reshape(2, 128).T
        cT = np.stack([f(c_prompt)[0], f(c_sample)[si]], axis=-1)
        vecs[:, 41:57] = cT.reshape(8, 128, 2).transpose(1, 0, 2).reshape(128, 16)
        m = dict(common)
        m["vecs"] = vecs
        m["xk0"] = np.ascontiguousarray(np.roll(xp, -c * QP, axis=0))
        m["xk1"] = np.ascontiguousarray(np.roll(xs[si], -hf * QS, axis=0))
        m["tabA0"] = np.ascontiguousarray(np.roll(tabA_p, -c * QP, axis=2))
        m["tabB0"] = np.ascontiguousarray(np.roll(tabB_p, -c * QP, axis=2))
        m["tabA1"] = np.ascontiguousarray(np.roll(tabA_s, -hf * QS, axis=2))
        m["tabB1"] = np.ascontiguousarray(np.roll(tabB_s, -hf * QS, axis=2))
        in_maps.append(m)
    y0 = [np.zeros((QP, D), np.float32) for _ in range(NC)]
    y1 = [np.zeros((QS, D), np.float32) for _ in range(NC)]
    for nc, segs in zip(ncs, launches):
        res = run_bass_kernel_spmd(nc, in_maps, core_ids=list(range(NC)))
        for (sg, qlo, qhi) in segs:
            Q = QP if sg == 0 else QS
            lo, hi = qlo * TB, (Q if qhi is None else qhi * TB)
            for c in range(NC):
                (y0 if sg == 0 else y1)[c][lo:hi] = res.results[c]["y%d" % sg][lo:hi]
    y_p = np.concatenate(y0, axis=0)[None]
    y_s = np.stack([np.concatenate([y1[2 * i], y1[2 * i + 1]], axis=0) for i in range(NC // 2)], axis=0)
    return y_p.astype(np.float32), y_s.astype(np.float32)


LAUNCHES = (((0, 0, None), (1, 0, None)),)


def kernel(**inputs):
    return _run(8, 2048, 2048, LAUNCHES, **inputs)
```

```python
import numpy as np
from contextlib import ExitStack
from functools import partial
import concourse.bass as bass
import concourse.mybir as mybir
from concourse.bass_utils import run_bass_kernel_spmd

F32 = mybir.dt.float32
BF16 = mybir.dt.bfloat16
ALU = mybir.AluOpType
AF = mybir.ActivationFunctionType

D = 1024
EPS = 1e-6
GRID_W = 64
ROPE_THETA = 10000.0
C_AQ, C_AK, C_AV, C_AG, C_BQ, C_BKV, C_BKR, C_BG, C_MA, C_MB = 0, 512, 640, 768, 1280, 1664, 1920, 1952, 2464, 3488
TB = 512
SKEW_A = True
HOIST_FE = True
RSQRT_LN = False
SAME_SYNC = True


class Res:
    __slots__ = ("name", "w", "rs")

    def __init__(self, name):
        self.name = name
        self.w = None
        self.rs = {}


class Prog:
    def __init__(self, nc, stack):
        self.nc = nc
        self.stack = stack
        self.ops = []

    def op(self, eng, name, kw, reads=(), writes=(), dma_key=None):
        idx = len(self.ops)
        ek = ("d", dma_key) if dma_key else ("e", eng)
        deps = set()
        for r in reads:
            if r.w is not None:
                deps.add(r.w)
        for w in writes:
            if w.w is not None:
                deps.add(w.w)
            deps.update(w.rs.values())
        for r in reads:
            r.rs[ek] = idx
        for w in writes:
            w.w = idx
            w.rs = {}
        deps.discard(idx)
        ndma = len(kw) if isinstance(kw, list) else 1
        self.ops.append(dict(eng=eng, name=name, kw=kw, deps=deps, ek=ek, ndma=ndma, sig=False, val=0))
        return idx

    def _skip(self, ek, eng):
        return ek[0] == "e" and ek[1] == eng and (eng == "pe" or not SAME_SYNC)

    def finalize(self):
        ops = self.ops
        for o in ops:
            red = {}
            for d in o["deps"]:
                ek = ops[d]["ek"]
                if ek not in red or red[ek] < d:
                    red[ek] = d
            o["rdeps"] = {ek: d for ek, d in red.items() if not self._skip(ek, o["eng"])}
            for d in o["rdeps"].values():
                ops[d]["sig"] = True
        cnt = {}
        for o in ops:
            ek = o["ek"]
            if ek[0] == "d":
                cnt[ek] = cnt.get(ek, 0) + 16 * o["ndma"]
                o["val"] = cnt[ek]
            elif o["sig"]:
                cnt[ek] = cnt.get(ek, 0) + 1
                o["val"] = cnt[ek]
        self.sems = {}
        for i, ek in enumerate(cnt):
            self.sems[ek] = self.stack.enter_context(self.nc.semaphore("s%d_%s" % (i, str(ek[1])[:12])))

    def replay(self, engname, eng):
        ops = self.ops
        seen = {}
        for o in ops:
            if o["eng"] != engname:
                continue
            for ek, d in o["rdeps"].items():
                v = ops[d]["val"]
                if seen.get(ek, 0) >= v:
                    continue
                eng.wait_ge(self.sems[ek], v)
                seen[ek] = v
            f = getattr(eng, o["name"])
            if isinstance(o["kw"], list):
                insts = [f(**k) for k in o["kw"]]
            else:
                insts = [f(**o["kw"])]
            if o["ek"][0] == "d":
                for ins in insts:
                    ins.then_inc(self.sems[o["ek"]], 16)
            elif o["sig"]:
                insts[0].then_inc(self.sems[o["ek"]], 1)

    def final_wait(self, eng):
        last = {}
        for o in self.ops:
            if o["ek"][0] == "d":
                last[o["ek"]] = o["val"]
        for ek, v in last.items():
            eng.wait_ge(self.sems[ek], v)


def _build(NC, QP, QS, segs=((0, 0, None), (1, 0, None))):
    NP = NC * QP
    NS = 2 * QS
    seqs = [dict(n=NP, q=QP), dict(n=NS, q=QS)]
    CK = min(1024, NP, NS)
    TPC = CK // 128
    for s in seqs:
        s["nch"] = s["n"] // CK
        assert s["n"] % CK == 0 and s["q"] % TB == 0 and s["n"] % TB == 0

    nc = bass.Bass("TRN2", target_bir_lowering=False)
    stack = ExitStack()
    P = Prog(nc, stack)

    def din(name, shape, dt=F32):
        return nc.dram_tensor(name, list(shape), dt, kind="ExternalInput").ap()

    def dscr(name, shape, dt=BF16):
        return nc.dram_tensor(name, list(shape), dt, kind="Internal").ap()

    xk = [din("xk0", [NP, D]), din("xk1", [NS, D])]
    tabA = [din("tabA0", [2, 64, NP]), din("tabA1", [2, 64, NS])]
    tabB = [din("tabB0", [2, 32, NP]), din("tabB1", [2, 32, NS])]
    vecs_d = din("vecs", [128, 64])
    rows_d = din("rows", [1, 2048])
    consts_d = din("consts", [128, 3, 128])
    ada_w = din("ada_w", [D, 3 * D])
    w_in = din("w_in", [D, 4512])
    b_q_up = din("b_q_up", [384, 768])
    b_kv_up = din("b_kv_up", [256, 1024])
    a_out = din("a_out", [512, D])
    b_out = din("b_out", [512, D])
    w_o = din("w_o", [D, D])
    outs = [nc.dram_tensor("y0", [QP, D], F32, kind="ExternalOutput").ap(),
            nc.dram_tensor("y1", [QS, D], F32, kind="ExternalOutput").ap()]

    NG = 13
    wst = dscr("wst", [NG, 128, 4096])
    r_wst = [Res("wst%d" % g) for g in range(NG)]
    for s, sq in enumerate(seqs):
        n = sq["n"]
        sq["KA"] = dscr("KA%d" % s, [128, n])
        sq["KBn"] = dscr("KBn%d" % s, [4, 128, n])
        sq["KR"] = dscr("KR%d" % s, [32, n])
        sq["VA"] = dscr("VA%d" % s, [2, sq["nch"], 128, TPC, 64])
        sq["VB"] = dscr("VB%d" % s, [8, sq["nch"], 128, TPC, 64])

    def sb(name, shape, dt):
        return stack.enter_context(nc.sbuf_tensor(name, list(shape), dt))

    ARENA = 85 * 1024
    arena = sb("arena", [128, ARENA // 2], BF16)

    class Carver:
        def __init__(self, off=0):
            self.off = off
            self.hi = off

        def get(self, shape, dt, parts=128):
            esz = 4 if dt == F32 else 2
            nel = 1
            for d_ in shape[1:]:
                nel *= d_
            nb = (nel * esz + 31) // 32 * 32
            v = arena[:, self.off // 2:(self.off + nel * esz) // 2]
            if dt == F32:
                v = v.bitcast(F32)
            if len(shape) == 3:
                v = v.rearrange("p (a b) -> p a b", b=shape[2])
            elif len(shape) == 4:
                v = v.rearrange("p (a b c) -> p a b c", b=shape[2], c=shape[3])
            self.off += nb
            assert self.off <= ARENA, self.off
            return v

    ps = stack.enter_context(nc.psum_tensor("ps", [128, 8, 512], F32))
    psr = [Res("ps%d" % b) for b in range(8)]
    bank_rr = [0]

    def bank():
        b = bank_rr[0]
        bank_rr[0] = (b + 1) % 8
        return b

    consts_b = sb("consts_b", [128, 3, 128], BF16)
    r_consts = Res("consts")
    ident = consts_b[:, 0, :]
    ones_b = consts_b[:, 1, :]
    bones_b = consts_b[:, 2, :]
    vecs = sb("vecs_sb", [128, 64], F32)
    r_vecs = Res("vecs")
    V_PREG, V_ADAB, V_GAQ, V_GAQR, V_GAK, V_GAKR, V_BQG, V_BKVG, V_CT = 0, 8, 32, 33, 34, 35, 36, 39, 41
    scT = sb("scT", [128, 8, 2], F32)
    aT = sb("aT", [128, 2, 8], F32)
    shT = sb("shT", [128, 2, 8], F32)
    r_mod = Res("mod")
    gp_bc = sb("gp_bc", [128, 2, 1024], F32)
    r_gp = Res("gp")
    wqup_sb = sb("wqup_sb", [128, 3, 1536], BF16)
    r_wqup = Res("wqup")
    NWB = 3
    wbuf = [sb("wbuf%d" % i, [128, 4096], BF16) for i in range(NWB)]
    r_wbuf = [Res("wbuf%d" % i) for i in range(NWB)]
    wb_rr = [0]
    NXT = 3
    xt = [sb("xt%d" % i, [128, D], F32) for i in range(NXT)]
    r_xt = [Res("xt%d" % i) for i in range(NXT)]
    xt_rr = [0]
    xn = [sb("xn%d" % i, [128, D], BF16) for i in range(2)]
    r_xn = [Res("xn%d" % i) for i in range(2)]
    xn_rr = [0]
    junk = sb("junk", [128, D], BF16)
    r_junk = Res("junk")
    ss = sb("ss", [128, 8], F32)
    r_ss = [Res("ss%d" % i) for i in range(8)]
    ss_rr = [0]
    st1 = sb("st1", [128, 8], F32)
    r_st1 = [Res("st1_%d" % i) for i in range(8)]
    st_rr = [0]
    hT = [sb("hT%d" % i, [128, 8, TB], BF16) for i in range(2)]
    r_hT = [[Res("hT%d_%d" % (i, c)) for c in range(8)] for i in range(2)]
    hT_rr = [0]
    tA = [sb("tA%d" % i, [128, 2, TB], F32) for i in range(2)]
    r_tA = [Res("tA%d" % i) for i in range(2)]
    tab_rr = [0]
    NTMP = 6
    tmp = [sb("tmp%d" % i, [128, TB], F32) for i in range(NTMP)]
    r_tmp = [Res("tmp%d" % i) for i in range(NTMP)]
    tmp_rr = [0]

    def gettmp():
        i = tmp_rr[0]
        tmp_rr[0] = (i + 1) % NTMP
        return tmp[i], r_tmp[i]

    NSQ = 4
    sqb = [sb("sqb%d" % i, [128, TB], BF16) for i in range(NSQ)]
    r_sqb = [Res("sqb%d" % i) for i in range(NSQ)]
    sq_rr = [0]

    def getsq():
        i = sq_rr[0]
        sq_rr[0] = (i + 1) % NSQ
        return sqb[i], r_sqb[i]

    cA = Carver(0)
    wk_sb = cA.get([128, 8, 704], BF16)
    wkvup_sb = cA.get([128, 2, 1024], BF16)
    r_wk, r_wkvup = Res("wk"), Res("wkvup")
    off_after_w = cA.off
    tBk = [cA.get([128, 2, TB], F32) for i in range(2)]
    ka_st = [cA.get([128, TB], BF16) for i in range(2)]
    kb_st = [cA.get([128, 4, TB], BF16) for i in range(2)]
    kr_st = [cA.get([128, TB], BF16) for i in range(2)]
    va_st = [cA.get([128, 2, 4, 64], BF16) for i in range(2)]
    vb_st = [cA.get([128, 8, 4, 64], BF16) for i in range(2)]
    kvn = [cA.get([128, 2, TB], BF16) for i in range(2)]
    r_tBk = [Res("tBk%d" % i) for i in range(2)]
    r_ka_st = [Res("ka_st%d" % i) for i in range(2)]
    r_kb_st = [Res("kb_st%d" % i) for i in range(2)]
    r_kr_st = [Res("kr_st%d" % i) for i in range(2)]
    r_va_st = [Res("va_st%d" % i) for i in range(2)]
    r_vb_st = [Res("vb_st%d" % i) for i in range(2)]
    r_kvn = [Res("kvn%d" % i) for i in range(2)]
    resA = r_tBk + r_ka_st + r_kb_st + r_kr_st + r_va_st + r_vb_st + r_kvn

    c0 = Carver(off_after_w)
    consts_f = c0.get([128, 3, 128], F32)
    cS = Carver(cA.off)
    stgs = [cS.get([128, 8, 512], F32) for _ in range(2)]
    r_stgs = [Res("stg0"), Res("stg1")]
    stg_rr = [0]
    rows = c0.get([128, 2048], F32)
    gprow = c0.get([128, 2, 1024], F32)
    onesrow = c0.get([128, 128], F32)
    r_rows, r_onesrow, r_cf, r_gprow = Res("rows"), Res("onesrow"), Res("cf"), Res("gprow")
    res0 = [r_rows, r_onesrow, r_cf, r_gprow]

    cB = Carver(0)
    tB = cB.get([128, 2, TB], F32)
    r_tB = Res("tB")
    qaT = cB.get([128, 8, TB], BF16)
    r_qaT = [Res("qaT%d" % i) for i in range(8)]
    qbT = cB.get([128, 8, TB], BF16)
    r_qbT = [Res("qbT%d" % i) for i in range(8)]
    agT = cB.get([128, 4, TB], BF16)
    r_agT = [Res("agT%d" % i) for i in range(4)]
    bgT = cB.get([128, 4, TB], BF16)
    r_bgT = [Res("bgT%d" % i) for i in range(4)]
    bqn = cB.get([128, 3, TB], BF16)
    r_bqn = [Res("bqn%d" % i) for i in range(3)]
    yaT = cB.get([128, 4, TB], BF16)
    ybT = cB.get([128, 4, TB], BF16)
    r_yaT = [[Res("yaT%d_%d" % (i, k)) for k in range(2)] for i in range(4)]
    r_ybT = [[Res("ybT%d_%d" % (i, k)) for k in range(2)] for i in range(4)]
    mergedT = cB.get([128, 8, TB], BF16)
    r_merged = [Res("merged%d" % i) for i in range(8)]
    NKB = 4
    kbuf = [cB.get([128, CK], BF16) for i in range(NKB)]
    vbuf = [cB.get([128, TPC, 192], BF16) for i in range(NKB)]
    r_kbuf = [Res("kbuf%d" % i) for i in range(NKB)]
    r_vbuf = [Res("vbuf%d" % i) for i in range(NKB)]
    kv_rr = [0]
    NPT = 3
    PT = [cB.get([128, 2 * TB], BF16) for i in range(NPT)]
    r_PT = [Res("PT%d" % i) for i in range(NPT)]
    pt_rr = [0]
    sg_rr = [0]
    o_rr = [0]
    rcs = [cB.get([128, TB], F32) for i in range(2)]
    r_rcs = [Res("rcs%d" % i) for i in range(2)]
    rc_rr = [0]
    smt = [cB.get([128, TB], BF16) for i in range(2)]
    r_smt = [Res("smt%d" % i) for i in range(2)]
    sm_rr = [0]
    ytile = cB.get([128, D], F32)
    r_ytile = Res("ytile")
    resB = ([r_tB] + r_qaT + r_qbT + r_agT + r_bgT + r_bqn + [x for l in r_yaT for x in l] + [x for l in r_ybT for x in l]
            + r_merged + r_kbuf + r_vbuf + r_PT + r_rcs + r_smt + [r_ytile])

    P.op("sp", "dma_start", dict(out=consts_f, in_=consts_d[:, :, :]), writes=[r_cf], dma_key="consts")
    P.op("sp", "dma_start", dict(out=vecs[:], in_=vecs_d[:, :]), writes=[r_vecs], dma_key="vecs")
    P.op("sp", "dma_start", dict(out=rows[0:1, :], in_=rows_d[:, :]), writes=[r_rows], dma_key="rows")
    P.op("dve", "tensor_copy", dict(out=consts_b[:], in_=consts_f), reads=[r_cf], writes=[r_consts])
    P.op("dve", "memset", dict(ap=onesrow[0:1, :], constant=1.0), writes=[r_onesrow])
    P.op("act", "activation", dict(out=scT[:].rearrange("p k s -> p (k s)"), in_=vecs[:, V_CT:V_CT + 16], func=AF.Silu),
         reads=[r_vecs], writes=[r_mod])

    adaw_v = ada_w.rearrange("(k p) n -> p k n", p=128)
    modb = bank()
    for g in range(6):
        sgi = stg_rr[0]
        stg_rr[0] = 1 - sgi
        stg, r_stg = stgs[sgi], r_stgs[sgi]
        P.op("sp", "dma_start", dict(out=stg, in_=adaw_v[:, :, g * 512:(g + 1) * 512]), writes=[r_stg], dma_key="stg%d" % sgi)
        if g < 4:
            for b4 in range(4):
                blk = g * 4 + b4
                for k in range(8):
                    P.op("pe", "matmul", dict(out=ps[:, modb, blk * 2:blk * 2 + 2], lhsT=stg[:, k, b4 * 128:(b4 + 1) * 128], rhs=scT[:, k, :],
                                              start=(k == 0), stop=(k == 7)), reads=[r_stg, r_mod], writes=[psr[modb]])
        else:
            g2 = g - 4
            for s in range(2):
                gb = bank()
                for k in range(8):
                    P.op("pe", "matmul", dict(out=ps[0:1, gb, :], lhsT=scT[:, k, s:s + 1], rhs=stg[:, k, :], start=(k == 0), stop=(k == 7)),
                         reads=[r_stg, r_mod], writes=[psr[gb]])
                P.op("dve", "tensor_tensor", dict(out=gprow[0:1, s, g2 * 512:(g2 + 1) * 512], in0=ps[0:1, gb, :],
                                                  in1=rows[0:1, g2 * 512:(g2 + 1) * 512], op=ALU.add),
                     reads=[psr[gb], r_rows], writes=[r_gprow])
                P.op("dve", "tensor_tensor", dict(out=gprow[0:1, s, g2 * 512:(g2 + 1) * 512], in0=gprow[0:1, s, g2 * 512:(g2 + 1) * 512],
                                                  in1=rows[0:1, 1024 + g2 * 512:1024 + (g2 + 1) * 512], op=ALU.mult),
                     reads=[r_gprow, r_rows], writes=[r_gprow])
    modv = ps[:, modb, 0:32].rearrange("p (b s) -> p s b", s=2)
    for s in range(2):
        P.op("dve", "tensor_tensor", dict(out=shT[:, s, :], in0=modv[:, s, 0:8], in1=vecs[:, V_ADAB:V_ADAB + 8], op=ALU.add),
             reads=[psr[modb], r_vecs], writes=[r_mod])
        P.op("dve", "tensor_tensor", dict(out=aT[:, s, :], in0=modv[:, s, 8:16], in1=vecs[:, V_ADAB + 8:V_ADAB + 16], op=ALU.add),
             reads=[psr[modb], r_vecs], writes=[r_mod])
        P.op("dve", "scalar_tensor_tensor", dict(out=aT[:, s, :], in0=aT[:, s, :], scalar=1.0, in1=vecs[:, V_PREG:V_PREG + 8],
                                                 op0=ALU.add, op1=ALU.mult), reads=[r_mod, r_vecs], writes=[r_mod])
    for s in range(2):
        for hf in range(2):
            gb = bank()
            P.op("pe", "matmul", dict(out=ps[:, gb, :], lhsT=onesrow[0:1, :], rhs=gprow[0:1, s, hf * 512:(hf + 1) * 512], start=True, stop=True),
                 reads=[r_gprow, r_onesrow], writes=[psr[gb]])
            P.op("dve", "tensor_copy", dict(out=gp_bc[:, s, hf * 512:(hf + 1) * 512], in_=ps[:, gb, :]), reads=[psr[gb]], writes=[r_gp])

    win_v = w_in.rearrange("(k p) n -> p k n", p=128)
    cv_rr = [0]

    def convert(loads, width, dst_ap, dst_res, kch=8):
        sgi = stg_rr[0]
        stg_rr[0] = 1 - sgi
        stg, r_stg = stgs[sgi], r_stgs[sgi]
        P.op("sp", "dma_start", [dict(out=stg[:, 0:kch, dc:dc + w], in_=src) for (dc, src, w) in loads], writes=[r_stg], dma_key="stg%d" % sgi)
        if cv_rr[0] % 2 == 0:
            P.op("dve", "tensor_copy", dict(out=dst_ap, in_=stg[:, 0:kch, 0:width]), reads=[r_stg], writes=[dst_res])
        else:
            P.op("act", "activation", dict(out=dst_ap, in_=stg[:, 0:kch, 0:width], func=AF.Copy), reads=[r_stg], writes=[dst_res])
        cv_rr[0] += 1

    def cols(c0_, w):
        return win_v[:, :, c0_:c0_ + w]

    def perm_loads(dst0, src_v, src_c0, nheads, dh):
        hf = dh // 2
        lds = []
        for h in range(nheads):
            lds.append((dst0 + h * dh, src_v[:, :, src_c0 + h * dh + hf:src_c0 + h * dh + dh], hf))
            lds.append((dst0 + h * dh + hf, src_v[:, :, src_c0 + h * dh:src_c0 + h * dh + hf], hf))
        return lds

    convert([(0, cols(C_AK, 128), 128)] + perm_loads(128, win_v, C_AK, 2, 64) + [(256, cols(C_BKV, 256), 256)], 512, wk_sb[:, :, 0:512], r_wk)
    convert([(0, cols(C_BKR, 32), 32)] + perm_loads(32, win_v, C_BKR, 1, 32) + [(64, cols(C_AV, 128), 128)], 192, wk_sb[:, :, 512:704], r_wk)

    late_jobs = []

    def wq_group(g, loads, width, kch=8):
        i = wb_rr[0]
        wb_rr[0] = (i + 1) % NWB
        wv = wbuf[i][:, 0:kch * width].rearrange("p (k n) -> p k n", n=width)
        convert(loads, width, wv, r_wbuf[i], kch=kch)
        P.op("pool", "dma_start", dict(out=wst[g, :, 0:kch * width], in_=wbuf[i][:, 0:kch * width]),
             reads=[r_wbuf[i]], writes=[r_wst[g]], dma_key="wbufS%d" % i)

    for g in range(2):
        lds = []
        for pi in range(2):
            pr = g * 2 + pi
            lds.append((pi * 256, cols(C_AQ + pr * 128, 128), 128))
            lds += perm_loads(pi * 256 + 128, win_v, C_AQ + pr * 128, 2, 64)
        late_jobs.append(partial(wq_group, g, lds, 512))
    late_jobs.append(partial(wq_group, 2, [(0, cols(C_AG, 512), 512)], 512))
    late_jobs.append(partial(wq_group, 3, [(0, cols(C_BQ, 384), 384)], 384))
    late_jobs.append(partial(wq_group, 4, [(0, cols(C_BG, 512), 512)], 512))
    late_jobs.append(partial(wq_group, 5, [(0, cols(C_MA, 512), 512)], 512))
    late_jobs.append(partial(wq_group, 6, [(0, cols(C_MA + 512, 512), 512)], 512))
    late_jobs.append(partial(wq_group, 7, [(0, cols(C_MB, 512), 512)], 512))
    late_jobs.append(partial(wq_group, 8, [(0, cols(C_MB + 512, 512), 512)], 512))
    wo_v = w_o.rearrange("(k p) n -> p k n", p=128)
    late_jobs.append(partial(wq_group, 9, [(0, wo_v[:, :, 0:512], 512)], 512))
    late_jobs.append(partial(wq_group, 10, [(0, wo_v[:, :, 512:1024], 512)], 512))
    def conv_ab(g, wv_):
        i = wb_rr[0]
        wb_rr[0] = (i + 1) % NWB
        wv = wbuf[i][:, :].rearrange("p (k n) -> p k n", n=1024)
        for hf in range(2):
            convert([(0, wv_[:, :, hf * 512:(hf + 1) * 512], 512)], 512, wv[:, :, hf * 512:(hf + 1) * 512], r_wbuf[i], kch=4)
        P.op("pool", "dma_start", dict(out=wst[g, :, :], in_=wbuf[i][:, :]), reads=[r_wbuf[i]], writes=[r_wst[g]], dma_key="wbufS%d" % i)

    late_jobs.append(partial(conv_ab, 11, a_out.rearrange("(k p) n -> p k n", p=128)))
    late_jobs.append(partial(conv_ab, 12, b_out.rearrange("(k p) n -> p k n", p=128)))
    kvu_v = b_kv_up.rearrange("(k p) (h t d) -> p k t h d", p=128, t=2, d=64)
    for t in range(2):
        sgi = stg_rr[0]
        stg_rr[0] = 1 - sgi
        stg, r_stg = stgs[sgi], r_stgs[sgi]
        P.op("sp", "dma_start", [dict(out=stg[:, k, :].rearrange("p (h d) -> p h d", d=64), in_=kvu_v[:, k, t, :, :]) for k in range(2)],
             writes=[r_stg], dma_key="stg%d" % sgi)
        P.op("dve", "tensor_copy", dict(out=wkvup_sb[:, :, t * 512:(t + 1) * 512], in_=stg[:, 0:2, :]), reads=[r_stg], writes=[r_wkvup])
    qu_v = b_q_up.rearrange("(k p) n -> p k n", p=128)
    late_jobs.append(partial(convert, [(0, qu_v[:, :, 0:512], 512)], 512, wqup_sb[:, :, 0:512], r_wqup, 3))
    late_jobs.append(partial(convert, [(0, qu_v[:, :, 512:768], 256)], 256, wqup_sb[:, :, 512:768], r_wqup, 3))
    for part in range(2):
        lds = []
        for hh in range(4):
            h = part * 4 + hh
            lds.append((hh * 96, qu_v[:, :, h * 96:h * 96 + 64], 64))
            lds.append((hh * 96 + 64, qu_v[:, :, h * 96 + 80:h * 96 + 96], 16))
            lds.append((hh * 96 + 80, qu_v[:, :, h * 96 + 64:h * 96 + 80], 16))
        late_jobs.append(partial(convert, lds, 384, wqup_sb[:, :, 768 + part * 384:768 + (part + 1) * 384], r_wqup, 3))

    P.op("pool", "nop", dict(), writes=res0 + resA)

    def rsqrt_ops(out_ap, in_ap, scale, in_res, out_res):
        if RSQRT_LN:
            P.op("act", "activation", dict(out=out_ap, in_=in_ap, func=AF.Ln, scale=scale, bias=EPS), reads=in_res, writes=[out_res])
            P.op("act", "activation", dict(out=out_ap, in_=out_ap, func=AF.Exp, scale=-0.5), reads=[out_res], writes=[out_res])
        else:
            P.op("act", "activation", dict(out=out_ap, in_=in_ap, func=AF.Sqrt, scale=scale, bias=EPS), reads=in_res, writes=[out_res])
            P.op("dve", "reciprocal", dict(out=out_ap, in_=out_ap), reads=[out_res], writes=[out_res])

    def front_end(s, tok0):
        hi = hT_rr[0]
        hT_rr[0] = (hi + 1) % 2
        tb4 = [bank() for _ in range(4)]
        for j in range(4):
            xi = xt_rr[0]
            xt_rr[0] = (xi + 1) % NXT
            P.op("sp", "dma_start", dict(out=xt[xi][:], in_=xk[s][tok0 + j * 128:tok0 + (j + 1) * 128, :]), writes=[r_xt[xi]], dma_key="xt%d" % xi)
            si = ss_rr[0]
            ss_rr[0] = (si + 1) % 8
            P.op("act", "activation", dict(out=junk[:], in_=xt[xi][:], func=AF.Square, accum_out=ss[:, si:si + 1]),
                 reads=[r_xt[xi]], writes=[r_junk, r_ss[si]])
            rsqrt_ops(ss[:, si:si + 1], ss[:, si:si + 1], 1.0 / D, [r_ss[si]], r_ss[si])
            ni = xn_rr[0]
            xn_rr[0] = (ni + 1) % 2
            P.op("dve", "tensor_scalar", dict(out=xn[ni][:], in0=xt[xi][:], scalar1=ss[:, si:si + 1], scalar2=None, op0=ALU.mult),
                 reads=[r_xt[xi], r_ss[si]], writes=[r_xn[ni]])
            for c in range(8):
                b = tb4[c // 2]
                off = (c % 2) * 512 + j * 128
                P.op("pe", "transpose", dict(out=ps[:, b, :].bitcast(BF16)[:, off:off + 128], in_=xn[ni][:, c * 128:(c + 1) * 128], identity=ident),
                     reads=[r_xn[ni], r_consts], writes=[psr[b]])
        for c in range(8):
            b = tb4[c // 2]
            off = (c % 2) * 512
            P.op("dve", "tensor_scalar", dict(out=hT[hi][:, c, :], in0=ps[:, b, :].bitcast(BF16)[:, off:off + 512], scalar1=aT[:, s, c:c + 1],
                                              scalar2=shT[:, s, c:c + 1], op0=ALU.mult, op1=ALU.add),
                 reads=[psr[b], r_mod], writes=[r_hT[hi][c]])
        return hi

    def proj8(hi, wv, c0_, M, w_res):
        b = bank()
        for c in range(8):
            P.op("pe", "matmul", dict(out=ps[0:M, b, :], lhsT=wv[:, c, c0_:c0_ + M], rhs=hT[hi][:, c, :], start=(c == 0), stop=(c == 7)),
                 reads=[w_res, r_hT[hi][c]], writes=[psr[b]])
        return b

    def load_tables(s, tok0, kind):
        ti = tab_rr[0]
        tab_rr[0] = (ti + 1) % 2
        src = tabA[s][:, :, tok0:tok0 + TB].rearrange("t p n -> p t n")
        P.op("sp", "dma_start", [dict(out=tA[ti][0:64, :, :], in_=src), dict(out=tA[ti][64:128, :, :], in_=src)],
             writes=[r_tA[ti]], dma_key="tA%d" % ti)
        gc, gr = (V_GAK, V_GAKR) if kind == "k" else (V_GAQ, V_GAQR)
        P.op("dve", "tensor_scalar", dict(out=tA[ti][:, 0, :], in0=tA[ti][:, 0, :], scalar1=vecs[:, gc:gc + 1], scalar2=None, op0=ALU.mult),
             reads=[r_tA[ti], r_vecs], writes=[r_tA[ti]])
        P.op("dve", "tensor_scalar", dict(out=tA[ti][:, 1, :], in0=tA[ti][:, 1, :], scalar1=vecs[:, gr:gr + 1], scalar2=None, op0=ALU.mult),
             reads=[r_tA[ti], r_vecs], writes=[r_tA[ti]])
        srcB = tabB[s][:, :, tok0:tok0 + TB].rearrange("t p n -> p t n")
        if kind == "k":
            P.op("sp", "dma_start", dict(out=tBk[ti][0:32, :, :], in_=srcB), writes=[r_tBk[ti]], dma_key="tBk%d" % ti)
        else:
            P.op("sp", "dma_start", dict(out=tB[64:96, :, :], in_=srcB), writes=[r_tB], dma_key="tB")
        return ti

    def rstd_from(msb, Dn):
        t, rt = gettmp()
        rsqrt_ops(t[:], ps[:, msb, :], 1.0 / Dn, [psr[msb]], rt)
        return t, rt

    def hnr_square(bx):
        sq, rsq = getsq()
        P.op("act", "activation", dict(out=sq[:], in_=ps[:, bx, :], func=AF.Square), reads=[psr[bx]], writes=[rsq])
        return sq, rsq

    def hnr_mix(bx, by, ti, after=()):
        t1, r1 = gettmp()
        t2, r2 = gettmp()
        P.op("dve", "tensor_tensor", dict(out=t1[:], in0=ps[:, bx, :], in1=tA[ti][:, 0, :], op=ALU.mult), reads=[psr[bx], r_tA[ti]] + list(after), writes=[r1])
        P.op("dve", "tensor_tensor", dict(out=t2[:], in0=ps[:, by, :], in1=tA[ti][:, 1, :], op=ALU.mult), reads=[psr[by], r_tA[ti]], writes=[r2])
        P.op("dve", "tensor_tensor", dict(out=t1[:], in0=t1[:], in1=t2[:], op=ALU.add), reads=[r1, r2], writes=[r1])
        return t1, r1

    def hnr_finish(sqp, mix, out_ap, out_res):
        sq, rsq = sqp
        t1, r1 = mix
        mb = bank()
        P.op("pe", "matmul", dict(out=ps[:, mb, :], lhsT=bones_b, rhs=sq[:], start=True, stop=True), reads=[rsq, r_consts], writes=[psr[mb]])
        rt, rrt = rstd_from(mb, 64)
        if isinstance(out_ap, list):
            for (oap, ores, psl) in out_ap:
                P.op("dve", "tensor_tensor", dict(out=oap, in0=t1[psl, :], in1=rt[psl, :], op=ALU.mult), reads=[r1, rrt], writes=[ores])
        else:
            P.op("dve", "tensor_tensor", dict(out=out_ap, in0=t1[:], in1=rt[:], op=ALU.mult), reads=[r1, rrt], writes=[out_res])

    def head_norm_rope(bx, by, ti, out_ap, out_res):
        sqp = hnr_square(bx)
        mix = hnr_mix(bx, by, ti, after=[sqp[1]])
        hnr_finish(sqp, mix, out_ap, out_res)

    def ln_square(banks):
        sqs = []
        for c, b in enumerate(banks):
            sq, rsq = getsq()
            P.op("act", "activation", dict(out=sq[:], in_=ps[:, b, :], func=AF.Square), reads=[psr[b]], writes=[rsq])
            sqs.append((sq, rsq))
        return sqs

    def ln_finish(banks, sqs, Dn, gcol, out_tile, out_res):
        mb = bank()
        for c, (sq, rsq) in enumerate(sqs):
            P.op("pe", "matmul", dict(out=ps[:, mb, :], lhsT=ones_b, rhs=sq[:], start=(c == 0), stop=(c == len(sqs) - 1)),
                 reads=[rsq, r_consts], writes=[psr[mb]])
        rt, rrt = rstd_from(mb, Dn)
        for c, b in enumerate(banks):
            P.op("dve", "scalar_tensor_tensor", dict(out=out_tile[:, c, :], in0=ps[:, b, :], scalar=vecs[:, gcol + c:gcol + c + 1], in1=rt[:],
                                                     op0=ALU.mult, op1=ALU.mult), reads=[psr[b], r_vecs, rrt], writes=[out_res[c]])

    def latent_norm(banks, Dn, gcol, out_tile, out_res):
        ln_finish(banks, ln_square(banks), Dn, gcol, out_tile, out_res)

    blocksA = [(s, tbk) for s in range(2) if s in [g[0] for g in segs] for tbk in range(seqs[s]["n"] // TB)]
    fe_done = {}

    def fe_A(bi):
        s_, tbk_ = blocksA[bi]
        hi_ = front_end(s_, tbk_ * TB)
        fe_done[bi] = (load_tables(s_, tbk_ * TB, "k"), hi_)

    for b_ in range(min(2, len(blocksA))):
        fe_A(b_)
    for bi, (s, tbk) in enumerate(blocksA):
        sq_ = seqs[s]
        if True:
            tok0 = tbk * TB
            pi = bi % 2
            ti, hi = fe_done.pop(bi)
            if late_jobs:
                late_jobs.pop(0)()
            bx = proj8(hi, wk_sb, 0, 128, r_wk)
            by = proj8(hi, wk_sb, 128, 128, r_wk)
            b0 = proj8(hi, wk_sb, 256, 128, r_wk)
            b1 = proj8(hi, wk_sb, 384, 128, r_wk)
            sqp = hnr_square(bx)
            sqs = ln_square([b0, b1])
            mix = hnr_mix(bx, by, ti, after=[sqp[1]])
            brx = proj8(hi, wk_sb, 512, 32, r_wk)
            bry = proj8(hi, wk_sb, 544, 32, r_wk)
            ch, sl = (tbk * 4) // TPC, (tbk * 4) % TPC
            ln_finish([b0, b1], sqs, 256, V_BKVG, kvn[pi], [r_kvn[pi], r_kvn[pi]])
            hnr_finish(sqp, mix, ka_st[pi][:], r_ka_st[pi])
            P.op("pool", "dma_start", dict(out=sq_["KA"][:, tok0:tok0 + TB], in_=ka_st[pi][:]), reads=[r_ka_st[pi]], dma_key="ka_st%d" % pi)
            va_ = bank()
            for j in range(4):
                for c in range(8):
                    P.op("pe", "matmul", dict(out=ps[:, va_, j * 128:(j + 1) * 128], lhsT=hT[hi][:, c, j * 128:(j + 1) * 128], rhs=wk_sb[:, c, 576:704],
                                              start=(c == 0), stop=(c == 7)), reads=[r_wk, r_hT[hi][c]], writes=[psr[va_]])
            t1, r1 = gettmp()
            t2, r2 = gettmp()
            P.op("dve", "tensor_tensor", dict(out=t1[0:32, :], in0=ps[0:32, brx, :], in1=tBk[ti][0:32, 0, :], op=ALU.mult),
                 reads=[psr[brx], r_tBk[ti]], writes=[r1])
            P.op("dve", "tensor_tensor", dict(out=t2[0:32, :], in0=ps[0:32, bry, :], in1=tBk[ti][0:32, 1, :], op=ALU.mult),
                 reads=[psr[bry], r_tBk[ti]], writes=[r2])
            P.op("dve", "tensor_tensor", dict(out=kr_st[pi][0:32, :], in0=t1[0:32, :], in1=t2[0:32, :], op=ALU.add),
                 reads=[r1, r2], writes=[r_kr_st[pi]])
            P.op("pool", "dma_start", dict(out=sq_["KR"][:, tok0:tok0 + TB], in_=kr_st[pi][0:32, :]), reads=[r_kr_st[pi]], dma_key="kr_st%d" % pi)
            P.op("dve", "tensor_copy", dict(out=va_st[pi][:].rearrange("p g j d -> p j g d"),
                                            in_=ps[:, va_, :].rearrange("p (j g d) -> p j g d", g=2, d=64)), reads=[psr[va_]], writes=[r_va_st[pi]])
            P.op("pool", "dma_start", dict(out=sq_["VA"][:, ch, :, sl:sl + 4, :].rearrange("g p j d -> p g (j d)"),
                                           in_=va_st[pi][:].rearrange("p g j d -> p g (j d)")), reads=[r_va_st[pi]], dma_key="va_st%d" % pi)
            if bi + 2 < len(blocksA):
                fe_A(bi + 2)
            for pr in range(4):
                kb_ = bank()
                for c in range(2):
                    P.op("pe", "matmul", dict(out=ps[:, kb_, :], lhsT=wkvup_sb[:, c, pr * 128:(pr + 1) * 128], rhs=kvn[pi][:, c, :],
                                              start=(c == 0), stop=(c == 1)), reads=[r_wkvup, r_kvn[pi]], writes=[psr[kb_]])
                P.op("act", "activation", dict(out=kb_st[pi][:, pr, :], in_=ps[:, kb_, :], func=AF.Copy), reads=[psr[kb_]], writes=[r_kb_st[pi]])
            P.op("pool", "dma_start", dict(out=sq_["KBn"][:, :, tok0:tok0 + TB].rearrange("r p n -> p r n"), in_=kb_st[pi][:]),
                 reads=[r_kb_st[pi]], dma_key="kb_st%d" % pi)
            for j in range(4):
                vb_ = bank()
                for c in range(2):
                    P.op("pe", "matmul", dict(out=ps[:, vb_, :], lhsT=kvn[pi][:, c, j * 128:(j + 1) * 128], rhs=wkvup_sb[:, c, 512:1024],
                                              start=(c == 0), stop=(c == 1)), reads=[r_wkvup, r_kvn[pi]], writes=[psr[vb_]])
                P.op("dve", "tensor_copy", dict(out=vb_st[pi][:, :, j, :], in_=ps[:, vb_, :].rearrange("p (h d) -> p h d", d=64)),
                     reads=[psr[vb_]], writes=[r_vb_st[pi]])
            P.op("pool", "dma_start", dict(out=sq_["VB"][:, ch, :, sl:sl + 4, :].rearrange("h p j d -> p h (j d)"),
                                           in_=vb_st[pi][:].rearrange("p h j d -> p h (j d)")), reads=[r_vb_st[pi]], dma_key="vb_st%d" % pi)

    while late_jobs:
        late_jobs.pop(0)()
    r_scr = Res("scratch")
    P.op("pool", "nop", dict(), writes=resA + [r_wk, r_wkvup] + r_stgs + resB + [r_scr])
    for i in range(NKB):
        P.op("pool", "memset", dict(ap=vbuf[i][:], constant=1.0), writes=[r_vbuf[i]])
    for h8 in range(8):
        P.op("pool", "memset", dict(ap=qaT[:, h8, :], constant=0.0), writes=[r_qaT[h8]])
    P.op("pool", "memset", dict(ap=tB[0:64, 0, :], constant=1.0), writes=[r_tB])
    P.op("pool", "memset", dict(ap=tB[0:64, 1, :], constant=0.0), writes=[r_tB])

    def wload(g, nel=4096):
        i = wb_rr[0]
        wb_rr[0] = (i + 1) % NWB
        P.op("sp", "dma_start", dict(out=wbuf[i][:, 0:nel], in_=wst[g, :, 0:nel]), reads=[r_wst[g]], writes=[r_wbuf[i]], dma_key="wbuf%d" % i)
        return i

    def wview(i, width=512):
        return wbuf[i][:, 0:8 * width].rearrange("p (k n) -> p k n", n=width)

    feB = {}
    for (s, qlo, qhi) in segs:
        sq_ = seqs[s]
        n, nch = sq_["n"], sq_["nch"]
        for qb in range(qlo, sq_["q"] // TB if qhi is None else qhi):
            tok0 = qb * TB
            if (s, qb) in feB:
                ti, hi = feB.pop((s, qb))
            else:
                ti, hi = load_tables(s, tok0, "q"), front_end(s, tok0)
            for g in range(2):
                wi = wload(g)
                for pi2 in range(2):
                    pr = g * 2 + pi2
                    bx = proj8(hi, wview(wi), pi2 * 256, 128, r_wbuf[wi])
                    by = proj8(hi, wview(wi), pi2 * 256 + 128, 128, r_wbuf[wi])
                    head_norm_rope(bx, by, ti, [(qaT[0:64, 2 * pr, :], r_qaT[2 * pr], slice(0, 64)),
                                                (qaT[64:128, 2 * pr + 1, :], r_qaT[2 * pr + 1], slice(64, 128))], None)
            wi = wload(2)
            for i4 in range(4):
                b = proj8(hi, wview(wi), i4 * 128, 128, r_wbuf[wi])
                P.op("act", "activation", dict(out=agT[:, i4, :], in_=ps[:, b, :], func=AF.Silu), reads=[psr[b]], writes=[r_agT[i4]])
            wi = wload(3, 8 * 384)
            bqb = [proj8(hi, wview(wi, 384), i3 * 128, 128, r_wbuf[wi]) for i3 in range(3)]
            latent_norm(bqb, 384, V_BQG, bqn, r_bqn)
            wi = wload(4)
            for i4 in range(4):
                b = proj8(hi, wview(wi), i4 * 128, 128, r_wbuf[wi])
                P.op("act", "activation", dict(out=bgT[:, i4, :], in_=ps[:, b, :], func=AF.Silu), reads=[psr[b]], writes=[r_bgT[i4]])
            for h in range(8):
                bx, by = bank(), bank()
                for (bb, off) in ((bx, 0), (by, 768)):
                    for c in range(3):
                        P.op("pe", "matmul", dict(out=ps[0:96, bb, :], lhsT=wqup_sb[:, c, off + h * 96:off + (h + 1) * 96], rhs=bqn[:, c, :],
                                                  start=(c == 0), stop=(c == 2)), reads=[r_wqup, r_bqn[c]], writes=[psr[bb]])
                t1, r1 = gettmp()
                t2, r2 = gettmp()
                P.op("dve", "tensor_tensor", dict(out=t1[0:96, :], in0=ps[0:96, bx, :], in1=tB[0:96, 0, :], op=ALU.mult),
                     reads=[psr[bx], r_tB], writes=[r1])
                P.op("dve", "tensor_tensor", dict(out=t2[0:96, :], in0=ps[0:96, by, :], in1=tB[0:96, 1, :], op=ALU.mult),
                     reads=[psr[by], r_tB], writes=[r2])
                P.op("dve", "tensor_tensor", dict(out=qbT[0:96, h, :], in0=t1[0:96, :], in1=t2[0:96, :], op=ALU.add),
                     reads=[r1, r2], writes=[r_qbT[h]])

            qlist = [(s2, q2) for (s2, lo2, hi2) in segs for q2 in range(lo2, seqs[s2]["q"] // TB if hi2 is None else hi2)]
            qpos = qlist.index((s, qb))
            if HOIST_FE and qpos + 1 < len(qlist):
                s2, q2 = qlist[qpos + 1]
                feB[(s2, q2)] = (load_tables(s2, q2 * TB, "q"), front_end(s2, q2 * TB))

            def kv_load(kind, idx, chk, sq_=sq_):
                i = kv_rr[0]
                kv_rr[0] = (i + 1) % NKB
                k0 = chk * CK
                if kind == "A":
                    src = sq_["KA"][idx * 64:(idx + 1) * 64, k0:k0 + CK]
                    P.op("sp", "dma_start", [dict(out=kbuf[i][0:64, :], in_=src), dict(out=kbuf[i][64:128, :], in_=src)],
                         reads=[r_scr], writes=[r_kbuf[i]], dma_key="kbuf%d" % i)
                    P.op("sp", "dma_start", dict(out=vbuf[i][:, :, 64:128], in_=sq_["VA"][idx, chk, :, :, :]),
                         reads=[r_scr], writes=[r_vbuf[i]], dma_key="vbuf%d" % i)
                else:
                    h = idx
                    P.op("sp", "dma_start", [dict(out=kbuf[i][0:64, :], in_=sq_["KBn"][h // 2, (h % 2) * 64:(h % 2) * 64 + 64, k0:k0 + CK]),
                                             dict(out=kbuf[i][64:96, :], in_=sq_["KR"][:, k0:k0 + CK])],
                         reads=[r_scr], writes=[r_kbuf[i]], dma_key="kbuf%d" % i)
                    P.op("sp", "dma_start", dict(out=vbuf[i][:, :, 64:128], in_=sq_["VB"][h, chk, :, :, :]),
                         reads=[r_scr], writes=[r_vbuf[i]], dma_key="vbuf%d" % i)
                return i

            units = [("A", g, [4 * g + k for k in range(4)]) for g in range(2)] + [("B", h, [h]) for h in range(8)]
            loads = [(u, chk) for u in range(len(units)) for chk in range(nch)]
            loaded = {}
            PF = 2
            nxt = 0
            pendq = []
            PVLAG = 2

            def emit_pv(st):
                for t in range(2):
                    P.op("pe", "matmul", dict(out=ps[:, st["ob"], :], lhsT=st["v"][t], rhs=PT[st["pt"]][:, t * TB:(t + 1) * TB],
                                              start=(st["first"] and t == 0), stop=(st["last"] and t == 1)),
                         reads=[r_vbuf[st["kvi"]], r_PT[st["pt"]]], writes=[psr[st["ob"]]])

            def finalize_head(kind, h, ob):
                par = h % 2
                nm = slice(64, 128) if par else slice(0, 64)
                dn = slice(0, 64) if par else slice(64, 128)
                ri = rc_rr[0]
                rc_rr[0] = (ri + 1) % 2
                P.op("dve", "reciprocal", dict(out=rcs[ri][dn, :], in_=ps[dn, ob, :]), reads=[psr[ob]], writes=[r_rcs[ri]])
                t1, r1 = gettmp()
                P.op("dve", "tensor_tensor", dict(out=t1[nm, :], in0=ps[nm, ob, :], in1=rcs[ri][dn, :], op=ALU.mult),
                     reads=[psr[ob], r_rcs[ri]], writes=[r1])
                gT, rg = (agT, r_agT) if kind == "A" else (bgT, r_bgT)
                yT, ry = (yaT, r_yaT) if kind == "A" else (ybT, r_ybT)
                P.op("pool", "tensor_tensor", dict(out=yT[nm, h // 2, :], in0=t1[nm, :], in1=gT[nm, h // 2, :], op=ALU.mult),
                     reads=[r1, rg[h // 2]], writes=[ry[h // 2][par]])

            li = 0
            for u, (kind, idx, heads) in enumerate(units):
                obs = {}
                for hh in heads:
                    obs[hh] = 4 + o_rr[0]
                    o_rr[0] = (o_rr[0] + 1) % 4
                scale = 0.125 if kind == "A" else 96 ** -0.5
                for chk in range(nch):
                    while nxt < len(loads) and nxt <= li + PF:
                        loaded[nxt] = kv_load(units[loads[nxt][0]][0], units[loads[nxt][0]][1], loads[nxt][1])
                        nxt += 1
                    kvi = loaded[li]
                    li += 1
                    for hh in heads:
                        par = hh % 2
                        if kind == "A":
                            ksl = slice(0, 128)
                            q_ap = qaT[:, hh, :]
                            q_res = r_qaT[hh]
                        else:
                            ksl = slice(0, 96)
                            q_ap = qbT[0:96, hh, :]
                            q_res = r_qbT[hh]
                        vcols = slice(0, 128) if par else slice(64, 192)
                        for gp in range(TPC // 2):
                            sg = sg_rr[0]
                            sg_rr[0] = (sg + 1) % 2
                            sb0 = sg * 2
                            pt = pt_rr[0]
                            pt_rr[0] = (pt + 1) % NPT
                            st = dict(ob=obs[hh], kvi=kvi, pt=pt, v=[vbuf[kvi][:, gp * 2 + t, vcols] for t in range(2)],
                                      first=(chk == 0 and gp == 0), last=(chk == nch - 1 and gp == TPC // 2 - 1))
                            for t in range(2):
                                kt = gp * 2 + t
                                P.op("pe", "matmul", dict(out=ps[:, sb0 + t, :], lhsT=kbuf[kvi][ksl, kt * 128:(kt + 1) * 128], rhs=q_ap,
                                                          start=True, stop=True), reads=[r_kbuf[kvi], q_res], writes=[psr[sb0 + t]])
                            P.op("act", "activation", dict(out=PT[pt][:], in_=ps[:, sb0:sb0 + 2, :].rearrange("p b n -> p (b n)"),
                                                           func=AF.Exp, scale=scale), reads=[psr[sb0], psr[sb0 + 1]], writes=[r_PT[pt]])
                            pendq.append((st, (kind, hh, obs[hh])))
                            if len(pendq) > PVLAG:
                                p0 = pendq.pop(0)
                                emit_pv(p0[0])
                                if p0[0]["last"]:
                                    finalize_head(*p0[1])
            while pendq:
                p0 = pendq.pop(0)
                emit_pv(p0[0])
                if p0[0]["last"]:
                    finalize_head(*p0[1])

            for (nm_, gw, g0, yT, ry) in (("a", 11, 5, yaT, r_yaT), ("b", 12, 7, ybT, r_ybT)):
                wo_ = wload(gw)
                wov = wbuf[wo_][:, :].rearrange("p (k n) -> p k n", n=1024)
                for cb in range(8):
                    if cb % 4 == 0:
                        wi = wload(g0 + cb // 4)
                    pa = bank()
                    for c in range(4):
                        P.op("pe", "matmul", dict(out=ps[:, pa, :], lhsT=wov[:, c, cb * 128:(cb + 1) * 128], rhs=yT[:, c, :],
                                                  start=(c == 0), stop=(c == 3)), reads=[r_wbuf[wo_], ry[c][0], ry[c][1]], writes=[psr[pa]])
                    pm = proj8(hi, wview(wi), (cb % 4) * 128, 128, r_wbuf[wi])
                    si = sm_rr[0]
                    sm_rr[0] = (si + 1) % 2
                    P.op("act", "activation", dict(out=smt[si][:], in_=ps[:, pm, :], func=AF.Sigmoid), reads=[psr[pm]], writes=[r_smt[si]])
                    if nm_ == "a":
                        P.op("dve", "tensor_tensor", dict(out=mergedT[:, cb, :], in0=ps[:, pa, :], in1=smt[si][:], op=ALU.mult),
                             reads=[psr[pa], r_smt[si]], writes=[r_merged[cb]])
                    else:
                        t1, r1 = gettmp()
                        P.op("dve", "tensor_tensor", dict(out=t1[:], in0=ps[:, pa, :], in1=smt[si][:], op=ALU.mult),
                             reads=[psr[pa], r_smt[si]], writes=[r1])
                        P.op("pool", "tensor_tensor", dict(out=mergedT[:, cb, :], in0=mergedT[:, cb, :], in1=t1[:], op=ALU.add),
                             reads=[r1, r_merged[cb]], writes=[r_merged[cb]])
            wo_i = [wload(9), wload(10)]
            for j in range(4):
                xi = xt_rr[0]
                xt_rr[0] = (xi + 1) % NXT
                P.op("sp", "dma_start", dict(out=xt[xi][:], in_=xk[s][tok0 + j * 128:tok0 + (j + 1) * 128, :]), writes=[r_xt[xi]], dma_key="xt%d" % xi)
                zb = [bank(), bank()]
                sis = []
                for hf in range(2):
                    for c in range(8):
                        P.op("pe", "matmul", dict(out=ps[:, zb[hf], :], lhsT=mergedT[:, c, j * 128:(j + 1) * 128], rhs=wview(wo_i[hf])[:, c, :],
                                                  start=(c == 0), stop=(c == 7)), reads=[r_merged[c], r_wbuf[wo_i[hf]]], writes=[psr[zb[hf]]])
                    si = st_rr[0]
                    st_rr[0] = (si + 1) % 8
                    sis.append(si)
                    P.op("act", "activation", dict(out=junk[:, 0:512], in_=ps[:, zb[hf], :], func=AF.Square, accum_out=st1[:, si:si + 1]),
                         reads=[psr[zb[hf]]], writes=[r_junk, r_st1[si]])
                s0, s1 = sis
                P.op("dve", "tensor_tensor", dict(out=st1[:, s0:s0 + 1], in0=st1[:, s0:s0 + 1], in1=st1[:, s1:s1 + 1], op=ALU.add),
                     reads=[r_st1[s0], r_st1[s1]], writes=[r_st1[s0]])
                rsqrt_ops(st1[:, s0:s0 + 1], st1[:, s0:s0 + 1], 1.0 / D, [r_st1[s0]], r_st1[s0])
                for hf in range(2):
                    P.op("dve", "scalar_tensor_tensor", dict(out=ytile[:, hf * 512:(hf + 1) * 512], in0=ps[:, zb[hf], :], scalar=st1[:, s0:s0 + 1],
                                                             in1=gp_bc[:, s, hf * 512:(hf + 1) * 512], op0=ALU.mult, op1=ALU.mult),
                         reads=[psr[zb[hf]], r_st1[s0], r_gp], writes=[r_ytile])
                P.op("pool", "tensor_tensor", dict(out=ytile[:], in0=ytile[:], in1=xt[xi][:], op=ALU.add), reads=[r_ytile, r_xt[xi]], writes=[r_ytile])
                P.op("pool", "dma_start", dict(out=outs[s][tok0 + j * 128:tok0 + (j + 1) * 128, :], in_=ytile[:]), reads=[r_ytile], dma_key="ytile")

    P.finalize()
    with nc.Block() as block:
        @block.tensor
        def _(e):
            P.replay("pe", e)

        @block.scalar
        def _(e):
            P.replay("act", e)

        @block.vector
        def _(e):
            P.replay("dve", e)

        @block.gpsimd
        def _(e):
            P.replay("pool", e)
            P.final_wait(e)

        @block.sync
        def _(e):
            P.replay("sp", e)
    stack.close()
    return nc


def _rope_tab(n, d_rot):
    rows = n // GRID_W
    row_ids = np.repeat(np.arange(rows, dtype=np.float32), GRID_W)
    col_ids = np.tile(np.arange(GRID_W, dtype=np.float32), rows)
    d_axis = d_rot // 2
    inv = (ROPE_THETA ** (-np.arange(0, d_axis, 2, dtype=np.float32) / d_axis)).astype(np.float32)
    ang = np.concatenate([row_ids[:, None] * inv, col_ids[:, None] * inv], axis=-1).astype(np.float32)
    cos = np.cos(ang).astype(np.float32).T
    sin = np.sin(ang).astype(np.float32).T
    c = np.concatenate([cos, cos], axis=0)
    s = np.concatenate([-sin, sin], axis=0)
    return np.stack([c, s], axis=0)


_NC_CACHE = {}


def _run(NC, QP, QS, launches, x_prompt, x_sample, c_prompt, c_sample, ada_w, ada_b, pre_norm_g, post_norm_g, w_in,
         a_q_norm_g, a_k_norm_g, b_q_norm_g, b_q_up, b_kv_norm_g, b_kv_up, a_out, b_out, w_o):
    f = lambda a: np.ascontiguousarray(np.asarray(a, dtype=np.float32))
    NP, NS = NC * QP, 2 * QS
    ncs = []
    for segs in launches:
        key = (NC, QP, QS, segs)
        if key not in _NC_CACHE:
            _NC_CACHE[key] = _build(NC, QP, QS, segs)
        ncs.append(_NC_CACHE[key])
    xp = f(x_prompt)[0]
    xs = f(x_sample)
    tabA_p, tabB_p = _rope_tab(NP, 64), _rope_tab(NP, 32)
    tabA_s, tabB_s = _rope_tab(NS, 64), _rope_tab(NS, 32)
    consts = np.zeros((128, 3, 128), np.float32)
    consts[:, 0, :] = np.eye(128, dtype=np.float32)
    consts[:, 1, :] = 1.0
    consts[0:64, 2, 0:64] = 1.0
    consts[64:128, 2, 64:128] = 1.0
    ada_b0 = f(ada_b)[0]
    gq, gk = f(a_q_norm_g)[0], f(a_k_norm_g)[0]
    rot = lambda g: np.concatenate([g[32:], g[:32]])
    rows = np.concatenate([ada_b0[2048:3072], f(post_norm_g)[0]])[None, :]
    common = dict(rows=f(rows), consts=consts, ada_w=f(ada_w)[0], w_in=f(w_in)[0], b_q_up=f(b_q_up)[0], b_kv_up=f(b_kv_up)[0],
                  a_out=f(a_out)[0], b_out=f(b_out)[0], w_o=f(w_o)[0])
    in_maps = []
    for c in range(NC):
        si, hf = c // 2, c % 2
        vecs = np.zeros((128, 64), np.float32)
        vecs[:, 0:8] = f(pre_norm_g)[0].reshape(8, 128).T
        vecs[:, 8:32] = ada_b0.reshape(24, 128).T
        vecs[:, 32] = np.tile(gq, 2)
        vecs[:, 33] = np.tile(rot(gq), 2)
        vecs[:, 34] = np.tile(gk, 2)
        vecs[:, 35] = np.tile(rot(gk), 2)
        vecs[:, 36:39] = f(b_q_norm_g)[0].reshape(3, 128).T
        vecs[:, 39:41] = f(b_kv_norm_g)[0].reshape(2, 128).T
        cT = np.stack([f(c_prompt)[0], f(c_sample)[si]], axis=-1)
        vecs[:, 41:57] = cT.reshape(8, 128, 2).transpose(1, 0, 2).reshape(128, 16)
        m = dict(common)
        m["vecs"] = vecs
        m["xk0"] = np.ascontiguousarray(np.roll(xp, -c * QP, axis=0))
        m["xk1"] = np.ascontiguousarray(np.roll(xs[si], -hf * QS, axis=0))
        m["tabA0"] = np.ascontiguousarray(np.roll(tabA_p, -c * QP, axis=2))
        m["tabB0"] = np.ascontiguousarray(np.roll(tabB_p, -c * QP, axis=2))
        m["tabA1"] = np.ascontiguousarray(np.roll(tabA_s, -hf * QS, axis=2))
        m["tabB1"] = np.ascontiguousarray(np.roll(tabB_s, -hf * QS, axis=2))
        in_maps.append(m)
    y0 = [np.zeros((QP, D), np.float32) for _ in range(NC)]
    y1 = [np.zeros((QS, D), np.float32) for _ in range(NC)]
    for nc, segs in zip(ncs, launches):
        res = run_bass_kernel_spmd(nc, in_maps, core_ids=list(range(NC)))
        for (sg, qlo, qhi) in segs:
            Q = QP if sg == 0 else QS
            lo, hi = qlo * TB, (Q if qhi is None else qhi * TB)
            for c in range(NC):
                (y0 if sg == 0 else y1)[c][lo:hi] = res.results[c]["y%d" % sg][lo:hi]
    y_p = np.concatenate(y0, axis=0)[None]
    y_s = np.stack([np.concatenate([y1[2 * i], y1[2 * i + 1]], axis=0) for i in range(NC // 2)], axis=0)
    return y_p.astype(np.float32), y_s.astype(np.float32)


LAUNCHES = (((0, 0, None), (1, 0, None)),)


def kernel(**inputs):
    return _run(8, 2048, 2048, LAUNCHES, **inputs)
```

```python
import numpy as np
from contextlib import ExitStack
from functools import partial
import concourse.bass as bass
import concourse.mybir as mybir
from concourse.bass_utils import run_bass_kernel_spmd

F32 = mybir.dt.float32
BF16 = mybir.dt.bfloat16
ALU = mybir.AluOpType
AF = mybir.ActivationFunctionType

D = 1024
EPS = 1e-6
GRID_W = 64
ROPE_THETA = 10000.0
C_AQ, C_AK, C_AV, C_AG, C_BQ, C_BKV, C_BKR, C_BG, C_MA, C_MB = 0, 512, 640, 768, 1280, 1664, 1920, 1952, 2464, 3488
TB = 512
SKEW_A = True
HOIST_FE = True
RSQRT_LN = False
SAME_SYNC = True


class Res:
    __slots__ = ("name", "w", "rs")

    def __init__(self, name):
        self.name = name
        self.w = None
        self.rs = {}


class Prog:
    def __init__(self, nc, stack):
        self.nc = nc
        self.stack = stack
        self.ops = []

    def op(self, eng, name, kw, reads=(), writes=(), dma_key=None):
        idx = len(self.ops)
        ek = ("d", dma_key) if dma_key else ("e", eng)
        deps = set()
        for r in reads:
            if r.w is not None:
                deps.add(r.w)
        for w in writes:
            if w.w is not None:
                deps.add(w.w)
            deps.update(w.rs.values())
        for r in reads:
            r.rs[ek] = idx
        for w in writes:
            w.w = idx
            w.rs = {}
        deps.discard(idx)
        ndma = len(kw) if isinstance(kw, list) else 1
        self.ops.append(dict(eng=eng, name=name, kw=kw, deps=deps, ek=ek, ndma=ndma, sig=False, val=0))
        return idx

    def _skip(self, ek, eng):
        return ek[0] == "e" and ek[1] == eng and (eng == "pe" or not SAME_SYNC)

    def finalize(self):
        ops = self.ops
        for o in ops:
            red = {}
            for d in o["deps"]:
                ek = ops[d]["ek"]
                if ek not in red or red[ek] < d:
                    red[ek] = d
            o["rdeps"] = {ek: d for ek, d in red.items() if not self._skip(ek, o["eng"])}
            for d in o["rdeps"].values():
                ops[d]["sig"] = True
        cnt = {}
        for o in ops:
            ek = o["ek"]
            if ek[0] == "d":
                cnt[ek] = cnt.get(ek, 0) + 16 * o["ndma"]
                o["val"] = cnt[ek]
            elif o["sig"]:
                cnt[ek] = cnt.get(ek, 0) + 1
                o["val"] = cnt[ek]
        self.sems = {}
        for i, ek in enumerate(cnt):
            self.sems[ek] = self.stack.enter_context(self.nc.semaphore("s%d_%s" % (i, str(ek[1])[:12])))

    def replay(self, engname, eng):
        ops = self.ops
        seen = {}
        for o in ops:
            if o["eng"] != engname:
                continue
            for ek, d in o["rdeps"].items():
                v = ops[d]["val"]
                if seen.get(ek, 0) >= v:
                    continue
                eng.wait_ge(self.sems[ek], v)
                seen[ek] = v
            f = getattr(eng, o["name"])
            if isinstance(o["kw"], list):
                insts = [f(**k) for k in o["kw"]]
            else:
                insts = [f(**o["kw"])]
            if o["ek"][0] == "d":
                for ins in insts:
                    ins.then_inc(self.sems[o["ek"]], 16)
            elif o["sig"]:
                insts[0].then_inc(self.sems[o["ek"]], 1)

    def final_wait(self, eng):
        last = {}
        for o in self.ops:
            if o["ek"][0] == "d":
                last[o["ek"]] = o["val"]
        for ek, v in last.items():
            eng.wait_ge(self.sems[ek], v)


def _build(NC, QP, QS, segs=((0, 0, None), (1, 0, None))):
    NP = NC * QP
    NS = 2 * QS
    seqs = [dict(n=NP, q=QP), dict(n=NS, q=QS)]
    CK = min(1024, NP, NS)
    TPC = CK // 128
    for s in seqs:
        s["nch"] = s["n"] // CK
        assert s["n"] % CK == 0 and s["q"] % TB == 0 and s["n"] % TB == 0

    nc = bass.Bass("TRN2", target_bir_lowering=False)
    stack = ExitStack()
    P = Prog(nc, stack)

    def din(name, shape, dt=F32):
        return nc.dram_tensor(name, list(shape), dt, kind="ExternalInput").ap()

    def dscr(name, shape, dt=BF16):
        return nc.dram_tensor(name, list(shape), dt, kind="Internal").ap()

    xk = [din("xk0", [NP, D]), din("xk1", [NS, D])]
    tabA = [din("tabA0", [2, 64, NP]), din("tabA1", [2, 64, NS])]
    tabB = [din("tabB0", [2, 32, NP]), din("tabB1", [2, 32, NS])]
    vecs_d = din("vecs", [128, 64])
    rows_d = din("rows", [1, 2048])
    consts_d = din("consts", [128, 3, 128])
    ada_w = din("ada_w", [D, 3 * D])
    w_in = din("w_in", [D, 4512])
    b_q_up = din("b_q_up", [384, 768])
    b_kv_up = din("b_kv_up", [256, 1024])
    a_out = din("a_out", [512, D])
    b_out = din("b_out", [512, D])
    w_o = din("w_o", [D, D])
    outs = [nc.dram_tensor("y0", [QP, D], F32, kind="ExternalOutput").ap(),
            nc.dram_tensor("y1", [QS, D], F32, kind="ExternalOutput").ap()]

    NG = 13
    wst = dscr("wst", [NG, 128, 4096])
    r_wst = [Res("wst%d" % g) for g in range(NG)]
    for s, sq in enumerate(seqs):
        n = sq["n"]
        sq["KA"] = dscr("KA%d" % s, [128, n])
        sq["KBn"] = dscr("KBn%d" % s, [4, 128, n])
        sq["KR"] = dscr("KR%d" % s, [32, n])
        sq["VA"] = dscr("VA%d" % s, [2, sq["nch"], 128, TPC, 64])
        sq["VB"] = dscr("VB%d" % s, [8, sq["nch"], 128, TPC, 64])

    def sb(name, shape, dt):
        return stack.enter_context(nc.sbuf_tensor(name, list(shape), dt))

    ARENA = 89 * 1024
    arena = sb("arena", [128, ARENA // 2], BF16)

    class Carver:
        def __init__(self, off=0):
            self.off = off
            self.hi = off

        def get(self, shape, dt, parts=128):
            esz = 4 if dt == F32 else 2
            nel = 1
            for d_ in shape[1:]:
                nel *= d_
            nb = (nel * esz + 31) // 32 * 32
            v = arena[:, self.off // 2:(self.off + nel * esz) // 2]
            if dt == F32:
                v = v.bitcast(F32)
            if len(shape) == 3:
                v = v.rearrange("p (a b) -> p a b", b=shape[2])
            elif len(shape) == 4:
                v = v.rearrange("p (a b c) -> p a b c", b=shape[2], c=shape[3])
            self.off += nb
            assert self.off <= ARENA, self.off
            return v

    ps = stack.enter_context(nc.psum_tensor("ps", [128, 8, 512], F32))
    psr = [Res("ps%d" % b) for b in range(8)]
    bank_rr = [0]

    def bank():
        b = bank_rr[0]
        bank_rr[0] = (b + 1) % 8
        return b

    consts_b = sb("consts_b", [128, 3, 128], BF16)
    r_consts = Res("consts")
    ident = consts_b[:, 0, :]
    ones_b = consts_b[:, 1, :]
    bones_b = consts_b[:, 2, :]
    vecs = sb("vecs_sb", [128, 64], F32)
    r_vecs = Res("vecs")
    V_PREG, V_ADAB, V_GAQ, V_GAQR, V_GAK, V_GAKR, V_BQG, V_BKVG, V_CT = 0, 8, 32, 33, 34, 35, 36, 39, 41
    scT = sb("scT", [128, 8, 2], F32)
    aT = sb("aT", [128, 2, 8], F32)
    shT = sb("shT", [128, 2, 8], F32)
    r_mod = Res("mod")
    gp_bc = sb("gp_bc", [128, 2, 1024], F32)
    r_gp = Res("gp")
    wqup_sb = sb("wqup_sb", [128, 3, 1536], BF16)
    r_wqup = Res("wqup")
    NWB = 3
    wbuf = [sb("wbuf%d" % i, [128, 4096], BF16) for i in range(NWB)]
    r_wbuf = [Res("wbuf%d" % i) for i in range(NWB)]
    wb_rr = [0]
    NXT = 3
    xt = [sb("xt%d" % i, [128, D], F32) for i in range(NXT)]
    r_xt = [Res("xt%d" % i) for i in range(NXT)]
    xt_rr = [0]
    xn = [sb("xn%d" % i, [128, D], BF16) for i in range(2)]
    r_xn = [Res("xn%d" % i) for i in range(2)]
    xn_rr = [0]
    junk = sb("junk", [128, D], BF16)
    r_junk = Res("junk")
    ss = sb("ss", [128, 8], F32)
    r_ss = [Res("ss%d" % i) for i in range(8)]
    ss_rr = [0]
    st1 = sb("st1", [128, 8], F32)
    r_st1 = [Res("st1_%d" % i) for i in range(8)]
    st_rr = [0]
    hT = [sb("hT%d" % i, [128, 8, TB], BF16) for i in range(2)]
    r_hT = [[Res("hT%d_%d" % (i, c)) for c in range(8)] for i in range(2)]
    hT_rr = [0]
    tA = [sb("tA%d" % i, [128, 2, TB], F32) for i in range(2)]
    r_tA = [Res("tA%d" % i) for i in range(2)]
    tab_rr = [0]
    NTMP = 6
    tmp = [sb("tmp%d" % i, [128, TB], F32) for i in range(NTMP)]
    r_tmp = [Res("tmp%d" % i) for i in range(NTMP)]
    tmp_rr = [0]

    def gettmp():
        i = tmp_rr[0]
        tmp_rr[0] = (i + 1) % NTMP
        return tmp[i], r_tmp[i]

    NSQ = 4
    sqb = [sb("sqb%d" % i, [128, TB], BF16) for i in range(NSQ)]
    r_sqb = [Res("sqb%d" % i) for i in range(NSQ)]
    sq_rr = [0]

    def getsq():
        i = sq_rr[0]
        sq_rr[0] = (i + 1) % NSQ
        return sqb[i], r_sqb[i]

    cA = Carver(0)
    wk_sb = cA.get([128, 8, 704], BF16)
    wkvup_sb = cA.get([128, 2, 1024], BF16)
    r_wk, r_wkvup = Res("wk"), Res("wkvup")
    off_after_w = cA.off
    tBk = [cA.get([128, 2, TB], F32) for i in range(2)]
    ka_st = [cA.get([128, TB], BF16) for i in range(2)]
    kb_st = [cA.get([128, 4, TB], BF16) for i in range(2)]
    kr_st = [cA.get([128, TB], BF16) for i in range(2)]
    va_st = [cA.get([128, 2, 4, 64], BF16) for i in range(2)]
    vb_st = [cA.get([128, 8, 4, 64], BF16) for i in range(2)]
    kvn = [cA.get([128, 2, TB], BF16) for i in range(2)]
    r_tBk = [Res("tBk%d" % i) for i in range(2)]
    r_ka_st = [Res("ka_st%d" % i) for i in range(2)]
    r_kb_st = [Res("kb_st%d" % i) for i in range(2)]
    r_kr_st = [Res("kr_st%d" % i) for i in range(2)]
    r_va_st = [Res("va_st%d" % i) for i in range(2)]
    r_vb_st = [Res("vb_st%d" % i) for i in range(2)]
    r_kvn = [Res("kvn%d" % i) for i in range(2)]
    resA = r_tBk + r_ka_st + r_kb_st + r_kr_st + r_va_st + r_vb_st + r_kvn

    c0 = Carver(off_after_w)
    consts_f = c0.get([128, 3, 128], F32)
    cS = Carver(cA.off)
    stgs = [cS.get([128, 8, 512], F32) for _ in range(2)]
    r_stgs = [Res("stg0"), Res("stg1")]
    stg_rr = [0]
    rows = c0.get([128, 2048], F32)
    gprow = c0.get([128, 2, 1024], F32)
    onesrow = c0.get([128, 128], F32)
    r_rows, r_onesrow, r_cf, r_gprow = Res("rows"), Res("onesrow"), Res("cf"), Res("gprow")
    res0 = [r_rows, r_onesrow, r_cf, r_gprow]

    cB = Carver(0)
    tB = cB.get([128, 2, TB], F32)
    r_tB = Res("tB")
    qaT = cB.get([128, 8, TB], BF16)
    r_qaT = [Res("qaT%d" % i) for i in range(8)]
    qbT = cB.get([128, 8, TB], BF16)
    r_qbT = [Res("qbT%d" % i) for i in range(8)]
    agT = cB.get([128, 4, TB], BF16)
    r_agT = [Res("agT%d" % i) for i in range(4)]
    bgT = cB.get([128, 4, TB], BF16)
    r_bgT = [Res("bgT%d" % i) for i in range(4)]
    bqn = cB.get([128, 3, TB], BF16)
    r_bqn = [Res("bqn%d" % i) for i in range(3)]
    yaT = cB.get([128, 4, TB], BF16)
    ybT = cB.get([128, 4, TB], BF16)
    r_yaT = [[Res("yaT%d_%d" % (i, k)) for k in range(2)] for i in range(4)]
    r_ybT = [[Res("ybT%d_%d" % (i, k)) for k in range(2)] for i in range(4)]
    mergedT = cB.get([128, 8, TB], BF16)
    r_merged = [Res("merged%d" % i) for i in range(8)]
    NKB = 4
    kbuf = [cB.get([128, CK], BF16) for i in range(NKB)]
    vbuf = [cB.get([128, TPC, 192], BF16) for i in range(NKB)]
    r_kbuf = [Res("kbuf%d" % i) for i in range(NKB)]
    r_vbuf = [Res("vbuf%d" % i) for i in range(NKB)]
    kv_rr = [0]
    NPT = 3
    PT = [cB.get([128, 2 * TB], BF16) for i in range(NPT)]
    r_PT = [Res("PT%d" % i) for i in range(NPT)]
    pt_rr = [0]
    sg_rr = [0]
    o_rr = [0]
    rcs = [cB.get([128, TB], F32) for i in range(2)]
    r_rcs = [Res("rcs%d" % i) for i in range(2)]
    rc_rr = [0]
    smt = [cB.get([128, TB], BF16) for i in range(2)]
    r_smt = [Res("smt%d" % i) for i in range(2)]
    sm_rr = [0]
    ytiles = [cB.get([128, D], F32) for _ in range(2)]
    r_ytiles = [Res("ytile0"), Res("ytile1")]
    yt_rr = [0]
    resB = ([r_tB] + r_qaT + r_qbT + r_agT + r_bgT + r_bqn + [x for l in r_yaT for x in l] + [x for l in r_ybT for x in l]
            + r_merged + r_kbuf + r_vbuf + r_PT + r_rcs + r_smt + r_ytiles)

    P.op("sp", "dma_start", dict(out=consts_f, in_=consts_d[:, :, :]), writes=[r_cf], dma_key="consts")
    P.op("sp", "dma_start", dict(out=vecs[:], in_=vecs_d[:, :]), writes=[r_vecs], dma_key="vecs")
    P.op("sp", "dma_start", dict(out=rows[0:1, :], in_=rows_d[:, :]), writes=[r_rows], dma_key="rows")
    P.op("dve", "tensor_copy", dict(out=consts_b[:], in_=consts_f), reads=[r_cf], writes=[r_consts])
    P.op("dve", "memset", dict(ap=onesrow[0:1, :], constant=1.0), writes=[r_onesrow])
    P.op("act", "activation", dict(out=scT[:].rearrange("p k s -> p (k s)"), in_=vecs[:, V_CT:V_CT + 16], func=AF.Silu),
         reads=[r_vecs], writes=[r_mod])

    adaw_v = ada_w.rearrange("(k p) n -> p k n", p=128)
    modb = bank()
    for g in range(6):
        sgi = stg_rr[0]
        stg_rr[0] = 1 - sgi
        stg, r_stg = stgs[sgi], r_stgs[sgi]
        P.op("sp", "dma_start", dict(out=stg, in_=adaw_v[:, :, g * 512:(g + 1) * 512]), writes=[r_stg], dma_key="stg%d" % sgi)
        if g < 4:
            for b4 in range(4):
                blk = g * 4 + b4
                for k in range(8):
                    P.op("pe", "matmul", dict(out=ps[:, modb, blk * 2:blk * 2 + 2], lhsT=stg[:, k, b4 * 128:(b4 + 1) * 128], rhs=scT[:, k, :],
                                              start=(k == 0), stop=(k == 7)), reads=[r_stg, r_mod], writes=[psr[modb]])
        else:
            g2 = g - 4
            for s in range(2):
                gb = bank()
                for k in range(8):
                    P.op("pe", "matmul", dict(out=ps[0:1, gb, :], lhsT=scT[:, k, s:s + 1], rhs=stg[:, k, :], start=(k == 0), stop=(k == 7)),
                         reads=[r_stg, r_mod], writes=[psr[gb]])
                P.op("dve", "tensor_tensor", dict(out=gprow[0:1, s, g2 * 512:(g2 + 1) * 512], in0=ps[0:1, gb, :],
                                                  in1=rows[0:1, g2 * 512:(g2 + 1) * 512], op=ALU.add),
                     reads=[psr[gb], r_rows], writes=[r_gprow])
                P.op("dve", "tensor_tensor", dict(out=gprow[0:1, s, g2 * 512:(g2 + 1) * 512], in0=gprow[0:1, s, g2 * 512:(g2 + 1) * 512],
                                                  in1=rows[0:1, 1024 + g2 * 512:1024 + (g2 + 1) * 512], op=ALU.mult),
                     reads=[r_gprow, r_rows], writes=[r_gprow])
    modv = ps[:, modb, 0:32].rearrange("p (b s) -> p s b", s=2)
    for s in range(2):
        P.op("dve", "tensor_tensor", dict(out=shT[:, s, :], in0=modv[:, s, 0:8], in1=vecs[:, V_ADAB:V_ADAB + 8], op=ALU.add),
             reads=[psr[modb], r_vecs], writes=[r_mod])
        P.op("dve", "tensor_tensor", dict(out=aT[:, s, :], in0=modv[:, s, 8:16], in1=vecs[:, V_ADAB + 8:V_ADAB + 16], op=ALU.add),
             reads=[psr[modb], r_vecs], writes=[r_mod])
        P.op("dve", "scalar_tensor_tensor", dict(out=aT[:, s, :], in0=aT[:, s, :], scalar=1.0, in1=vecs[:, V_PREG:V_PREG + 8],
                                                 op0=ALU.add, op1=ALU.mult), reads=[r_mod, r_vecs], writes=[r_mod])
    for s in range(2):
        for hf in range(2):
            gb = bank()
            P.op("pe", "matmul", dict(out=ps[:, gb, :], lhsT=onesrow[0:1, :], rhs=gprow[0:1, s, hf * 512:(hf + 1) * 512], start=True, stop=True),
                 reads=[r_gprow, r_onesrow], writes=[psr[gb]])
            P.op("dve", "tensor_copy", dict(out=gp_bc[:, s, hf * 512:(hf + 1) * 512], in_=ps[:, gb, :]), reads=[psr[gb]], writes=[r_gp])

    win_v = w_in.rearrange("(k p) n -> p k n", p=128)
    cv_rr = [0]

    def convert(loads, width, dst_ap, dst_res, kch=8):
        sgi = stg_rr[0]
        stg_rr[0] = 1 - sgi
        stg, r_stg = stgs[sgi], r_stgs[sgi]
        P.op("sp", "dma_start", [dict(out=stg[:, 0:kch, dc:dc + w], in_=src) for (dc, src, w) in loads], writes=[r_stg], dma_key="stg%d" % sgi)
        if cv_rr[0] % 2 == 0:
            P.op("dve", "tensor_copy", dict(out=dst_ap, in_=stg[:, 0:kch, 0:width]), reads=[r_stg], writes=[dst_res])
        else:
            P.op("act", "activation", dict(out=dst_ap, in_=stg[:, 0:kch, 0:width], func=AF.Copy), reads=[r_stg], writes=[dst_res])
        cv_rr[0] += 1

    def cols(c0_, w):
        return win_v[:, :, c0_:c0_ + w]

    def perm_loads(dst0, src_v, src_c0, nheads, dh):
        hf = dh // 2
        lds = []
        for h in range(nheads):
            lds.append((dst0 + h * dh, src_v[:, :, src_c0 + h * dh + hf:src_c0 + h * dh + dh], hf))
            lds.append((dst0 + h * dh + hf, src_v[:, :, src_c0 + h * dh:src_c0 + h * dh + hf], hf))
        return lds

    convert([(0, cols(C_AK, 128), 128)] + perm_loads(128, win_v, C_AK, 2, 64) + [(256, cols(C_BKV, 256), 256)], 512, wk_sb[:, :, 0:512], r_wk)
    convert([(0, cols(C_BKR, 32), 32)] + perm_loads(32, win_v, C_BKR, 1, 32) + [(64, cols(C_AV, 128), 128)], 192, wk_sb[:, :, 512:704], r_wk)

    late_jobs = []

    def wq_group(g, loads, width, kch=8):
        i = wb_rr[0]
        wb_rr[0] = (i + 1) % NWB
        wv = wbuf[i][:, 0:kch * width].rearrange("p (k n) -> p k n", n=width)
        convert(loads, width, wv, r_wbuf[i], kch=kch)
        P.op("pool", "dma_start", dict(out=wst[g, :, 0:kch * width], in_=wbuf[i][:, 0:kch * width]),
             reads=[r_wbuf[i]], writes=[r_wst[g]], dma_key="wbufS%d" % i)

    for g in range(2):
        lds = []
        for pi in range(2):
            pr = g * 2 + pi
            lds.append((pi * 256, cols(C_AQ + pr * 128, 128), 128))
            lds += perm_loads(pi * 256 + 128, win_v, C_AQ + pr * 128, 2, 64)
        late_jobs.append(partial(wq_group, g, lds, 512))
    late_jobs.append(partial(wq_group, 2, [(0, cols(C_AG, 512), 512)], 512))
    late_jobs.append(partial(wq_group, 3, [(0, cols(C_BQ, 384), 384)], 384))
    late_jobs.append(partial(wq_group, 4, [(0, cols(C_BG, 512), 512)], 512))
    late_jobs.append(partial(wq_group, 5, [(0, cols(C_MA, 512), 512)], 512))
    late_jobs.append(partial(wq_group, 6, [(0, cols(C_MA + 512, 512), 512)], 512))
    late_jobs.append(partial(wq_group, 7, [(0, cols(C_MB, 512), 512)], 512))
    late_jobs.append(partial(wq_group, 8, [(0, cols(C_MB + 512, 512), 512)], 512))
    wo_v = w_o.rearrange("(k p) n -> p k n", p=128)
    late_jobs.append(partial(wq_group, 9, [(0, wo_v[:, :, 0:512], 512)], 512))
    late_jobs.append(partial(wq_group, 10, [(0, wo_v[:, :, 512:1024], 512)], 512))
    def conv_ab(g, wv_):
        i = wb_rr[0]
        wb_rr[0] = (i + 1) % NWB
        wv = wbuf[i][:, :].rearrange("p (k n) -> p k n", n=1024)
        for hf in range(2):
            convert([(0, wv_[:, :, hf * 512:(hf + 1) * 512], 512)], 512, wv[:, :, hf * 512:(hf + 1) * 512], r_wbuf[i], kch=4)
        P.op("pool", "dma_start", dict(out=wst[g, :, :], in_=wbuf[i][:, :]), reads=[r_wbuf[i]], writes=[r_wst[g]], dma_key="wbufS%d" % i)

    late_jobs.append(partial(conv_ab, 11, a_out.rearrange("(k p) n -> p k n", p=128)))
    late_jobs.append(partial(conv_ab, 12, b_out.rearrange("(k p) n -> p k n", p=128)))
    kvu_v = b_kv_up.rearrange("(k p) (h t d) -> p k t h d", p=128, t=2, d=64)
    for t in range(2):
        sgi = stg_rr[0]
        stg_rr[0] = 1 - sgi
        stg, r_stg = stgs[sgi], r_stgs[sgi]
        P.op("sp", "dma_start", [dict(out=stg[:, k, :].rearrange("p (h d) -> p h d", d=64), in_=kvu_v[:, k, t, :, :]) for k in range(2)],
             writes=[r_stg], dma_key="stg%d" % sgi)
        P.op("dve", "tensor_copy", dict(out=wkvup_sb[:, :, t * 512:(t + 1) * 512], in_=stg[:, 0:2, :]), reads=[r_stg], writes=[r_wkvup])
    qu_v = b_q_up.rearrange("(k p) n -> p k n", p=128)
    late_jobs.append(partial(convert, [(0, qu_v[:, :, 0:512], 512)], 512, wqup_sb[:, :, 0:512], r_wqup, 3))
    late_jobs.append(partial(convert, [(0, qu_v[:, :, 512:768], 256)], 256, wqup_sb[:, :, 512:768], r_wqup, 3))
    for part in range(2):
        lds = []
        for hh in range(4):
            h = part * 4 + hh
            lds.append((hh * 96, qu_v[:, :, h * 96:h * 96 + 64], 64))
            lds.append((hh * 96 + 64, qu_v[:, :, h * 96 + 80:h * 96 + 96], 16))
            lds.append((hh * 96 + 80, qu_v[:, :, h * 96 + 64:h * 96 + 80], 16))
        late_jobs.append(partial(convert, lds, 384, wqup_sb[:, :, 768 + part * 384:768 + (part + 1) * 384], r_wqup, 3))

    P.op("pool", "nop", dict(), writes=res0 + resA)

    def rsqrt_ops(out_ap, in_ap, scale, in_res, out_res):
        if RSQRT_LN:
            P.op("act", "activation", dict(out=out_ap, in_=in_ap, func=AF.Ln, scale=scale, bias=EPS), reads=in_res, writes=[out_res])
            P.op("act", "activation", dict(out=out_ap, in_=out_ap, func=AF.Exp, scale=-0.5), reads=[out_res], writes=[out_res])
        else:
            P.op("act", "activation", dict(out=out_ap, in_=in_ap, func=AF.Sqrt, scale=scale, bias=EPS), reads=in_res, writes=[out_res])
            P.op("dve", "reciprocal", dict(out=out_ap, in_=out_ap), reads=[out_res], writes=[out_res])

    def front_end(s, tok0):
        hi = hT_rr[0]
        hT_rr[0] = (hi + 1) % 2
        tb4 = [bank() for _ in range(4)]
        for j in range(4):
            xi = xt_rr[0]
            xt_rr[0] = (xi + 1) % NXT
            P.op("sp", "dma_start", dict(out=xt[xi][:], in_=xk[s][tok0 + j * 128:tok0 + (j + 1) * 128, :]), writes=[r_xt[xi]], dma_key="xt%d" % xi)
            si = ss_rr[0]
            ss_rr[0] = (si + 1) % 8
            P.op("act", "activation", dict(out=junk[:], in_=xt[xi][:], func=AF.Square, accum_out=ss[:, si:si + 1]),
                 reads=[r_xt[xi]], writes=[r_junk, r_ss[si]])
            rsqrt_ops(ss[:, si:si + 1], ss[:, si:si + 1], 1.0 / D, [r_ss[si]], r_ss[si])
            ni = xn_rr[0]
            xn_rr[0] = (ni + 1) % 2
            P.op("dve", "tensor_scalar", dict(out=xn[ni][:], in0=xt[xi][:], scalar1=ss[:, si:si + 1], scalar2=None, op0=ALU.mult),
                 reads=[r_xt[xi], r_ss[si]], writes=[r_xn[ni]])
            for c in range(8):
                b = tb4[c // 2]
                off = (c % 2) * 512 + j * 128
                P.op("pe", "transpose", dict(out=ps[:, b, :].bitcast(BF16)[:, off:off + 128], in_=xn[ni][:, c * 128:(c + 1) * 128], identity=ident),
                     reads=[r_xn[ni], r_consts], writes=[psr[b]])
        for c in range(8):
            b = tb4[c // 2]
            off = (c % 2) * 512
            P.op("dve", "tensor_scalar", dict(out=hT[hi][:, c, :], in0=ps[:, b, :].bitcast(BF16)[:, off:off + 512], scalar1=aT[:, s, c:c + 1],
                                              scalar2=shT[:, s, c:c + 1], op0=ALU.mult, op1=ALU.add),
                 reads=[psr[b], r_mod], writes=[r_hT[hi][c]])
        return hi

    def proj8(hi, wv, c0_, M, w_res):
        b = bank()
        for c in range(8):
            P.op("pe", "matmul", dict(out=ps[0:M, b, :], lhsT=wv[:, c, c0_:c0_ + M], rhs=hT[hi][:, c, :], start=(c == 0), stop=(c == 7)),
                 reads=[w_res, r_hT[hi][c]], writes=[psr[b]])
        return b

    def load_tables(s, tok0, kind):
        ti = tab_rr[0]
        tab_rr[0] = (ti + 1) % 2
        src = tabA[s][:, :, tok0:tok0 + TB].rearrange("t p n -> p t n")
        P.op("sp", "dma_start", [dict(out=tA[ti][0:64, :, :], in_=src), dict(out=tA[ti][64:128, :, :], in_=src)],
             writes=[r_tA[ti]], dma_key="tA%d" % ti)
        gc, gr = (V_GAK, V_GAKR) if kind == "k" else (V_GAQ, V_GAQR)
        P.op("dve", "tensor_scalar", dict(out=tA[ti][:, 0, :], in0=tA[ti][:, 0, :], scalar1=vecs[:, gc:gc + 1], scalar2=None, op0=ALU.mult),
             reads=[r_tA[ti], r_vecs], writes=[r_tA[ti]])
        P.op("dve", "tensor_scalar", dict(out=tA[ti][:, 1, :], in0=tA[ti][:, 1, :], scalar1=vecs[:, gr:gr + 1], scalar2=None, op0=ALU.mult),
             reads=[r_tA[ti], r_vecs], writes=[r_tA[ti]])
        srcB = tabB[s][:, :, tok0:tok0 + TB].rearrange("t p n -> p t n")
        if kind == "k":
            P.op("sp", "dma_start", dict(out=tBk[ti][0:32, :, :], in_=srcB), writes=[r_tBk[ti]], dma_key="tBk%d" % ti)
        else:
            P.op("sp", "dma_start", dict(out=tB[64:96, :, :], in_=srcB), writes=[r_tB], dma_key="tB")
        return ti

    def rstd_from(msb, Dn):
        t, rt = gettmp()
        rsqrt_ops(t[:], ps[:, msb, :], 1.0 / Dn, [psr[msb]], rt)
        return t, rt

    def hnr_square(bx):
        sq, rsq = getsq()
        P.op("act", "activation", dict(out=sq[:], in_=ps[:, bx, :], func=AF.Square), reads=[psr[bx]], writes=[rsq])
        return sq, rsq

    def hnr_mix(bx, by, ti, after=()):
        t1, r1 = gettmp()
        t2, r2 = gettmp()
        P.op("dve", "tensor_tensor", dict(out=t1[:], in0=ps[:, bx, :], in1=tA[ti][:, 0, :], op=ALU.mult), reads=[psr[bx], r_tA[ti]] + list(after), writes=[r1])
        P.op("dve", "tensor_tensor", dict(out=t2[:], in0=ps[:, by, :], in1=tA[ti][:, 1, :], op=ALU.mult), reads=[psr[by], r_tA[ti]], writes=[r2])
        P.op("dve", "tensor_tensor", dict(out=t1[:], in0=t1[:], in1=t2[:], op=ALU.add), reads=[r1, r2], writes=[r1])
        return t1, r1

    def hnr_finish(sqp, mix, out_ap, out_res):
        sq, rsq = sqp
        t1, r1 = mix
        mb = bank()
        P.op("pe", "matmul", dict(out=ps[:, mb, :], lhsT=bones_b, rhs=sq[:], start=True, stop=True), reads=[rsq, r_consts], writes=[psr[mb]])
        rt, rrt = rstd_from(mb, 64)
        if isinstance(out_ap, list):
            for (oap, ores, psl) in out_ap:
                P.op("dve", "tensor_tensor", dict(out=oap, in0=t1[psl, :], in1=rt[psl, :], op=ALU.mult), reads=[r1, rrt], writes=[ores])
        else:
            P.op("dve", "tensor_tensor", dict(out=out_ap, in0=t1[:], in1=rt[:], op=ALU.mult), reads=[r1, rrt], writes=[out_res])

    def head_norm_rope(bx, by, ti, out_ap, out_res):
        sqp = hnr_square(bx)
        mix = hnr_mix(bx, by, ti, after=[sqp[1]])
        hnr_finish(sqp, mix, out_ap, out_res)

    def ln_square(banks):
        sqs = []
        for c, b in enumerate(banks):
            sq, rsq = getsq()
            P.op("act", "activation", dict(out=sq[:], in_=ps[:, b, :], func=AF.Square), reads=[psr[b]], writes=[rsq])
            sqs.append((sq, rsq))
        return sqs

    def ln_finish(banks, sqs, Dn, gcol, out_tile, out_res):
        mb = bank()
        for c, (sq, rsq) in enumerate(sqs):
            P.op("pe", "matmul", dict(out=ps[:, mb, :], lhsT=ones_b, rhs=sq[:], start=(c == 0), stop=(c == len(sqs) - 1)),
                 reads=[rsq, r_consts], writes=[psr[mb]])
        rt, rrt = rstd_from(mb, Dn)
        for c, b in enumerate(banks):
            P.op("dve", "scalar_tensor_tensor", dict(out=out_tile[:, c, :], in0=ps[:, b, :], scalar=vecs[:, gcol + c:gcol + c + 1], in1=rt[:],
                                                     op0=ALU.mult, op1=ALU.mult), reads=[psr[b], r_vecs, rrt], writes=[out_res[c]])

    def latent_norm(banks, Dn, gcol, out_tile, out_res):
        ln_finish(banks, ln_square(banks), Dn, gcol, out_tile, out_res)

    blocksA = [(s, tbk) for s in range(2) if s in [g[0] for g in segs] for tbk in range(seqs[s]["n"] // TB)]
    fe_done = {}

    def fe_A(bi):
        s_, tbk_ = blocksA[bi]
        hi_ = front_end(s_, tbk_ * TB)
        fe_done[bi] = (load_tables(s_, tbk_ * TB, "k"), hi_)

    for b_ in range(min(2, len(blocksA))):
        fe_A(b_)
    for bi, (s, tbk) in enumerate(blocksA):
        sq_ = seqs[s]
        if True:
            tok0 = tbk * TB
            pi = bi % 2
            ti, hi = fe_done.pop(bi)
            if late_jobs:
                late_jobs.pop(0)()
            bx = proj8(hi, wk_sb, 0, 128, r_wk)
            by = proj8(hi, wk_sb, 128, 128, r_wk)
            b0 = proj8(hi, wk_sb, 256, 128, r_wk)
            b1 = proj8(hi, wk_sb, 384, 128, r_wk)
            sqp = hnr_square(bx)
            sqs = ln_square([b0, b1])
            mix = hnr_mix(bx, by, ti, after=[sqp[1]])
            brx = proj8(hi, wk_sb, 512, 32, r_wk)
            bry = proj8(hi, wk_sb, 544, 32, r_wk)
            ch, sl = (tbk * 4) // TPC, (tbk * 4) % TPC
            ln_finish([b0, b1], sqs, 256, V_BKVG, kvn[pi], [r_kvn[pi], r_kvn[pi]])
            hnr_finish(sqp, mix, ka_st[pi][:], r_ka_st[pi])
            P.op("pool", "dma_start", dict(out=sq_["KA"][:, tok0:tok0 + TB], in_=ka_st[pi][:]), reads=[r_ka_st[pi]], dma_key="ka_st%d" % pi)
            va_ = bank()
            for j in range(4):
                for c in range(8):
                    P.op("pe", "matmul", dict(out=ps[:, va_, j * 128:(j + 1) * 128], lhsT=hT[hi][:, c, j * 128:(j + 1) * 128], rhs=wk_sb[:, c, 576:704],
                                              start=(c == 0), stop=(c == 7)), reads=[r_wk, r_hT[hi][c]], writes=[psr[va_]])
            t1, r1 = gettmp()
            t2, r2 = gettmp()
            P.op("dve", "tensor_tensor", dict(out=t1[0:32, :], in0=ps[0:32, brx, :], in1=tBk[ti][0:32, 0, :], op=ALU.mult),
                 reads=[psr[brx], r_tBk[ti]], writes=[r1])
            P.op("dve", "tensor_tensor", dict(out=t2[0:32, :], in0=ps[0:32, bry, :], in1=tBk[ti][0:32, 1, :], op=ALU.mult),
                 reads=[psr[bry], r_tBk[ti]], writes=[r2])
            P.op("dve", "tensor_tensor", dict(out=kr_st[pi][0:32, :], in0=t1[0:32, :], in1=t2[0:32, :], op=ALU.add),
                 reads=[r1, r2], writes=[r_kr_st[pi]])
            P.op("pool", "dma_start", dict(out=sq_["KR"][:, tok0:tok0 + TB], in_=kr_st[pi][0:32, :]), reads=[r_kr_st[pi]], dma_key="kr_st%d" % pi)
            P.op("dve", "tensor_copy", dict(out=va_st[pi][:].rearrange("p g j d -> p j g d"),
                                            in_=ps[:, va_, :].rearrange("p (j g d) -> p j g d", g=2, d=64)), reads=[psr[va_]], writes=[r_va_st[pi]])
            P.op("pool", "dma_start", dict(out=sq_["VA"][:, ch, :, sl:sl + 4, :].rearrange("g p j d -> p g (j d)"),
                                           in_=va_st[pi][:].rearrange("p g j d -> p g (j d)")), reads=[r_va_st[pi]], dma_key="va_st%d" % pi)
            if bi + 2 < len(blocksA):
                fe_A(bi + 2)
            for pr in range(4):
                kb_ = bank()
                for c in range(2):
                    P.op("pe", "matmul", dict(out=ps[:, kb_, :], lhsT=wkvup_sb[:, c, pr * 128:(pr + 1) * 128], rhs=kvn[pi][:, c, :],
                                              start=(c == 0), stop=(c == 1)), reads=[r_wkvup, r_kvn[pi]], writes=[psr[kb_]])
                P.op("act", "activation", dict(out=kb_st[pi][:, pr, :], in_=ps[:, kb_, :], func=AF.Copy), reads=[psr[kb_]], writes=[r_kb_st[pi]])
            P.op("pool", "dma_start", dict(out=sq_["KBn"][:, :, tok0:tok0 + TB].rearrange("r p n -> p r n"), in_=kb_st[pi][:]),
                 reads=[r_kb_st[pi]], dma_key="kb_st%d" % pi)
            for j in range(4):
                vb_ = bank()
                for c in range(2):
                    P.op("pe", "matmul", dict(out=ps[:, vb_, :], lhsT=kvn[pi][:, c, j * 128:(j + 1) * 128], rhs=wkvup_sb[:, c, 512:1024],
                                              start=(c == 0), stop=(c == 1)), reads=[r_wkvup, r_kvn[pi]], writes=[psr[vb_]])
                P.op("dve", "tensor_copy", dict(out=vb_st[pi][:, :, j, :], in_=ps[:, vb_, :].rearrange("p (h d) -> p h d", d=64)),
                     reads=[psr[vb_]], writes=[r_vb_st[pi]])
            P.op("pool", "dma_start", dict(out=sq_["VB"][:, ch, :, sl:sl + 4, :].rearrange("h p j d -> p h (j d)"),
                                           in_=vb_st[pi][:].rearrange("p h j d -> p h (j d)")), reads=[r_vb_st[pi]], dma_key="vb_st%d" % pi)

    while late_jobs:
        late_jobs.pop(0)()
    r_scr = Res("scratch")
    P.op("pool", "nop", dict(), writes=resA + [r_wk, r_wkvup] + r_stgs + resB + [r_scr])
    for i in range(NKB):
        P.op("pool", "memset", dict(ap=vbuf[i][:], constant=1.0), writes=[r_vbuf[i]])
    for h8 in range(8):
        P.op("pool", "memset", dict(ap=qaT[:, h8, :], constant=0.0), writes=[r_qaT[h8]])
    P.op("pool", "memset", dict(ap=tB[0:64, 0, :], constant=1.0), writes=[r_tB])
    P.op("pool", "memset", dict(ap=tB[0:64, 1, :], constant=0.0), writes=[r_tB])

    def wload(g, nel=4096):
        i = wb_rr[0]
        wb_rr[0] = (i + 1) % NWB
        P.op("sp", "dma_start", dict(out=wbuf[i][:, 0:nel], in_=wst[g, :, 0:nel]), reads=[r_wst[g]], writes=[r_wbuf[i]], dma_key="wbuf%d" % i)
        return i

    def wview(i, width=512):
        return wbuf[i][:, 0:8 * width].rearrange("p (k n) -> p k n", n=width)

    feB = {}
    for (s, qlo, qhi) in segs:
        sq_ = seqs[s]
        n, nch = sq_["n"], sq_["nch"]
        for qb in range(qlo, sq_["q"] // TB if qhi is None else qhi):
            tok0 = qb * TB
            if (s, qb) in feB:
                ti, hi = feB.pop((s, qb))
            else:
                ti, hi = load_tables(s, tok0, "q"), front_end(s, tok0)
            for g in range(2):
                wi = wload(g)
                for pi2 in range(2):
                    pr = g * 2 + pi2
                    bx = proj8(hi, wview(wi), pi2 * 256, 128, r_wbuf[wi])
                    by = proj8(hi, wview(wi), pi2 * 256 + 128, 128, r_wbuf[wi])
                    head_norm_rope(bx, by, ti, [(qaT[0:64, 2 * pr, :], r_qaT[2 * pr], slice(0, 64)),
                                                (qaT[64:128, 2 * pr + 1, :], r_qaT[2 * pr + 1], slice(64, 128))], None)
            wi = wload(2)
            for i4 in range(4):
                b = proj8(hi, wview(wi), i4 * 128, 128, r_wbuf[wi])
                P.op("act", "activation", dict(out=agT[:, i4, :], in_=ps[:, b, :], func=AF.Silu), reads=[psr[b]], writes=[r_agT[i4]])
            wi = wload(3, 8 * 384)
            bqb = [proj8(hi, wview(wi, 384), i3 * 128, 128, r_wbuf[wi]) for i3 in range(3)]
            latent_norm(bqb, 384, V_BQG, bqn, r_bqn)
            wi = wload(4)
            for i4 in range(4):
                b = proj8(hi, wview(wi), i4 * 128, 128, r_wbuf[wi])
                P.op("act", "activation", dict(out=bgT[:, i4, :], in_=ps[:, b, :], func=AF.Silu), reads=[psr[b]], writes=[r_bgT[i4]])
            for h in range(8):
                bx, by = bank(), bank()
                for (bb, off) in ((bx, 0), (by, 768)):
                    for c in range(3):
                        P.op("pe", "matmul", dict(out=ps[0:96, bb, :], lhsT=wqup_sb[:, c, off + h * 96:off + (h + 1) * 96], rhs=bqn[:, c, :],
                                                  start=(c == 0), stop=(c == 2)), reads=[r_wqup, r_bqn[c]], writes=[psr[bb]])
                t1, r1 = gettmp()
                t2, r2 = gettmp()
                P.op("dve", "tensor_tensor", dict(out=t1[0:96, :], in0=ps[0:96, bx, :], in1=tB[0:96, 0, :], op=ALU.mult),
                     reads=[psr[bx], r_tB], writes=[r1])
                P.op("dve", "tensor_tensor", dict(out=t2[0:96, :], in0=ps[0:96, by, :], in1=tB[0:96, 1, :], op=ALU.mult),
                     reads=[psr[by], r_tB], writes=[r2])
                P.op("dve", "tensor_tensor", dict(out=qbT[0:96, h, :], in0=t1[0:96, :], in1=t2[0:96, :], op=ALU.add),
                     reads=[r1, r2], writes=[r_qbT[h]])

            qlist = [(s2, q2) for (s2, lo2, hi2) in segs for q2 in range(lo2, seqs[s2]["q"] // TB if hi2 is None else hi2)]
            qpos = qlist.index((s, qb))
            if HOIST_FE and qpos + 1 < len(qlist):
                s2, q2 = qlist[qpos + 1]
                feB[(s2, q2)] = (load_tables(s2, q2 * TB, "q"), front_end(s2, q2 * TB))

            def kv_load(kind, idx, chk, sq_=sq_):
                i = kv_rr[0]
                kv_rr[0] = (i + 1) % NKB
                k0 = chk * CK
                if kind == "A":
                    src = sq_["KA"][idx * 64:(idx + 1) * 64, k0:k0 + CK]
                    P.op("sp", "dma_start", [dict(out=kbuf[i][0:64, :], in_=src), dict(out=kbuf[i][64:128, :], in_=src)],
                         reads=[r_scr], writes=[r_kbuf[i]], dma_key="kbuf%d" % i)
                    P.op("sp", "dma_start", dict(out=vbuf[i][:, :, 64:128], in_=sq_["VA"][idx, chk, :, :, :]),
                         reads=[r_scr], writes=[r_vbuf[i]], dma_key="vbuf%d" % i)
                else:
                    h = idx
                    P.op("sp", "dma_start", [dict(out=kbuf[i][0:64, :], in_=sq_["KBn"][h // 2, (h % 2) * 64:(h % 2) * 64 + 64, k0:k0 + CK]),
                                             dict(out=kbuf[i][64:96, :], in_=sq_["KR"][:, k0:k0 + CK])],
                         reads=[r_scr], writes=[r_kbuf[i]], dma_key="kbuf%d" % i)
                    P.op("sp", "dma_start", dict(out=vbuf[i][:, :, 64:128], in_=sq_["VB"][h, chk, :, :, :]),
                         reads=[r_scr], writes=[r_vbuf[i]], dma_key="vbuf%d" % i)
                return i

            units = [("A", g, [4 * g + k for k in range(4)]) for g in range(2)] + [("B", h, [h]) for h in range(8)]
            loads = [(u, chk) for u in range(len(units)) for chk in range(nch)]
            loaded = {}
            PF = 2
            nxt = 0
            pendq = []
            PVLAG = 2

            def emit_pv(st):
                for t in range(2):
                    P.op("pe", "matmul", dict(out=ps[:, st["ob"], :], lhsT=st["v"][t], rhs=PT[st["pt"]][:, t * TB:(t + 1) * TB],
                                              start=(st["first"] and t == 0), stop=(st["last"] and t == 1)),
                         reads=[r_vbuf[st["kvi"]], r_PT[st["pt"]]], writes=[psr[st["ob"]]])

            def finalize_head(kind, h, ob):
                par = h % 2
                nm = slice(64, 128) if par else slice(0, 64)
                dn = slice(0, 64) if par else slice(64, 128)
                ri = rc_rr[0]
                rc_rr[0] = (ri + 1) % 2
                P.op("dve", "reciprocal", dict(out=rcs[ri][dn, :], in_=ps[dn, ob, :]), reads=[psr[ob]], writes=[r_rcs[ri]])
                t1, r1 = gettmp()
                P.op("dve", "tensor_tensor", dict(out=t1[nm, :], in0=ps[nm, ob, :], in1=rcs[ri][dn, :], op=ALU.mult),
                     reads=[psr[ob], r_rcs[ri]], writes=[r1])
                gT, rg = (agT, r_agT) if kind == "A" else (bgT, r_bgT)
                yT, ry = (yaT, r_yaT) if kind == "A" else (ybT, r_ybT)
                P.op("pool", "tensor_tensor", dict(out=yT[nm, h // 2, :], in0=t1[nm, :], in1=gT[nm, h // 2, :], op=ALU.mult),
                     reads=[r1, rg[h // 2]], writes=[ry[h // 2][par]])

            li = 0
            for u, (kind, idx, heads) in enumerate(units):
                obs = {}
                for hh in heads:
                    obs[hh] = 4 + o_rr[0]
                    o_rr[0] = (o_rr[0] + 1) % 4
                scale = 0.125 if kind == "A" else 96 ** -0.5
                for chk in range(nch):
                    while nxt < len(loads) and nxt <= li + PF:
                        loaded[nxt] = kv_load(units[loads[nxt][0]][0], units[loads[nxt][0]][1], loads[nxt][1])
                        nxt += 1
                    kvi = loaded[li]
                    li += 1
                    for hh in heads:
                        par = hh % 2
                        if kind == "A":
                            ksl = slice(0, 128)
                            q_ap = qaT[:, hh, :]
                            q_res = r_qaT[hh]
                        else:
                            ksl = slice(0, 96)
                            q_ap = qbT[0:96, hh, :]
                            q_res = r_qbT[hh]
                        vcols = slice(0, 128) if par else slice(64, 192)
                        for gp in range(TPC // 2):
                            sg = sg_rr[0]
                            sg_rr[0] = (sg + 1) % 2
                            sb0 = sg * 2
                            pt = pt_rr[0]
                            pt_rr[0] = (pt + 1) % NPT
                            st = dict(ob=obs[hh], kvi=kvi, pt=pt, v=[vbuf[kvi][:, gp * 2 + t, vcols] for t in range(2)],
                                      first=(chk == 0 and gp == 0), last=(chk == nch - 1 and gp == TPC // 2 - 1))
                            for t in range(2):
                                kt = gp * 2 + t
                                P.op("pe", "matmul", dict(out=ps[:, sb0 + t, :], lhsT=kbuf[kvi][ksl, kt * 128:(kt + 1) * 128], rhs=q_ap,
                                                          start=True, stop=True), reads=[r_kbuf[kvi], q_res], writes=[psr[sb0 + t]])
                            P.op("act", "activation", dict(out=PT[pt][:], in_=ps[:, sb0:sb0 + 2, :].rearrange("p b n -> p (b n)"),
                                                           func=AF.Exp, scale=scale), reads=[psr[sb0], psr[sb0 + 1]], writes=[r_PT[pt]])
                            pendq.append((st, (kind, hh, obs[hh])))
                            if len(pendq) > PVLAG:
                                p0 = pendq.pop(0)
                                emit_pv(p0[0])
                                if p0[0]["last"]:
                                    finalize_head(*p0[1])
            while pendq:
                p0 = pendq.pop(0)
                emit_pv(p0[0])
                if p0[0]["last"]:
                    finalize_head(*p0[1])

            for (nm_, gw, g0, yT, ry) in (("a", 11, 5, yaT, r_yaT), ("b", 12, 7, ybT, r_ybT)):
                wo_ = wload(gw)
                wov = wbuf[wo_][:, :].rearrange("p (k n) -> p k n", n=1024)
                for cb in range(8):
                    if cb % 4 == 0:
                        wi = wload(g0 + cb // 4)
                    pa = bank()
                    for c in range(4):
                        P.op("pe", "matmul", dict(out=ps[:, pa, :], lhsT=wov[:, c, cb * 128:(cb + 1) * 128], rhs=yT[:, c, :],
                                                  start=(c == 0), stop=(c == 3)), reads=[r_wbuf[wo_], ry[c][0], ry[c][1]], writes=[psr[pa]])
                    pm = proj8(hi, wview(wi), (cb % 4) * 128, 128, r_wbuf[wi])
                    si = sm_rr[0]
                    sm_rr[0] = (si + 1) % 2
                    P.op("act", "activation", dict(out=smt[si][:], in_=ps[:, pm, :], func=AF.Sigmoid), reads=[psr[pm]], writes=[r_smt[si]])
                    if nm_ == "a":
                        P.op("dve", "tensor_tensor", dict(out=mergedT[:, cb, :], in0=ps[:, pa, :], in1=smt[si][:], op=ALU.mult),
                             reads=[psr[pa], r_smt[si]], writes=[r_merged[cb]])
                    else:
                        t1, r1 = gettmp()
                        P.op("dve", "tensor_tensor", dict(out=t1[:], in0=ps[:, pa, :], in1=smt[si][:], op=ALU.mult),
                             reads=[psr[pa], r_smt[si]], writes=[r1])
                        P.op("pool", "tensor_tensor", dict(out=mergedT[:, cb, :], in0=mergedT[:, cb, :], in1=t1[:], op=ALU.add),
                             reads=[r1, r_merged[cb]], writes=[r_merged[cb]])
            wo_i = [wload(9), wload(10)]
            for j in range(4):
                xi = xt_rr[0]
                xt_rr[0] = (xi + 1) % NXT
                P.op("sp", "dma_start", dict(out=xt[xi][:], in_=xk[s][tok0 + j * 128:tok0 + (j + 1) * 128, :]), writes=[r_xt[xi]], dma_key="xt%d" % xi)
                zb = [bank(), bank()]
                sis = []
                for hf in range(2):
                    for c in range(8):
                        P.op("pe", "matmul", dict(out=ps[:, zb[hf], :], lhsT=mergedT[:, c, j * 128:(j + 1) * 128], rhs=wview(wo_i[hf])[:, c, :],
                                                  start=(c == 0), stop=(c == 7)), reads=[r_merged[c], r_wbuf[wo_i[hf]]], writes=[psr[zb[hf]]])
                    si = st_rr[0]
                    st_rr[0] = (si + 1) % 8
                    sis.append(si)
                    P.op("act", "activation", dict(out=junk[:, 0:512], in_=ps[:, zb[hf], :], func=AF.Square, accum_out=st1[:, si:si + 1]),
                         reads=[psr[zb[hf]]], writes=[r_junk, r_st1[si]])
                s0, s1 = sis
                P.op("dve", "tensor_tensor", dict(out=st1[:, s0:s0 + 1], in0=st1[:, s0:s0 + 1], in1=st1[:, s1:s1 + 1], op=ALU.add),
                     reads=[r_st1[s0], r_st1[s1]], writes=[r_st1[s0]])
                rsqrt_ops(st1[:, s0:s0 + 1], st1[:, s0:s0 + 1], 1.0 / D, [r_st1[s0]], r_st1[s0])
                yi = yt_rr[0]
                yt_rr[0] = 1 - yi
                ytile, r_ytile = ytiles[yi], r_ytiles[yi]
                for hf in range(2):
                    P.op("dve", "scalar_tensor_tensor", dict(out=ytile[:, hf * 512:(hf + 1) * 512], in0=ps[:, zb[hf], :], scalar=st1[:, s0:s0 + 1],
                                                             in1=gp_bc[:, s, hf * 512:(hf + 1) * 512], op0=ALU.mult, op1=ALU.mult),
                         reads=[psr[zb[hf]], r_st1[s0], r_gp], writes=[r_ytile])
                P.op("dve", "tensor_tensor", dict(out=ytile[:], in0=ytile[:], in1=xt[xi][:], op=ALU.add), reads=[r_ytile, r_xt[xi]], writes=[r_ytile])
                P.op("pool", "dma_start", dict(out=outs[s][tok0 + j * 128:tok0 + (j + 1) * 128, :], in_=ytile[:]), reads=[r_ytile],
                     dma_key="ytile%d" % yi)

    P.finalize()
    with nc.Block() as block:
        @block.tensor
        def _(e):
            P.replay("pe", e)

        @block.scalar
        def _(e):
            P.replay("act", e)

        @block.vector
        def _(e):
            P.replay("dve", e)

        @block.gpsimd
        def _(e):
            P.replay("pool", e)
            P.final_wait(e)

        @block.sync
        def _(e):
            P.replay("sp", e)
    stack.close()
    return nc


def _rope_tab(n, d_rot):
    rows = n // GRID_W
    row_ids = np.repeat(np.arange(rows, dtype=np.float32), GRID_W)
    col_ids = np.tile(np.arange(GRID_W, dtype=np.float32), rows)
    d_axis = d_rot // 2
    inv = (ROPE_THETA ** (-np.arange(0, d_axis, 2, dtype=np.float32) / d_axis)).astype(np.float32)
    ang = np.concatenate([row_ids[:, None] * inv, col_ids[:, None] * inv], axis=-1).astype(np.float32)
    cos = np.cos(ang).astype(np.float32).T
    sin = np.sin(ang).astype(np.float32).T
    c = np.concatenate([cos, cos], axis=0)
    s = np.concatenate([-sin, sin], axis=0)
    return np.stack([c, s], axis=0)


_NC_CACHE = {}


def _run(NC, QP, QS, launches, x_prompt, x_sample, c_prompt, c_sample, ada_w, ada_b, pre_norm_g, post_norm_g, w_in,
         a_q_norm_g, a_k_norm_g, b_q_norm_g, b_q_up, b_kv_norm_g, b_kv_up, a_out, b_out, w_o):
    f = lambda a: np.ascontiguousarray(np.asarray(a, dtype=np.float32))
    NP, NS = NC * QP, 2 * QS
    ncs = []
    for segs in launches:
        key = (NC, QP, QS, segs)
        if key not in _NC_CACHE:
            _NC_CACHE[key] = _build(NC, QP, QS, segs)
        ncs.append(_NC_CACHE[key])
    xp = f(x_prompt)[0]
    xs = f(x_sample)
    tabA_p, tabB_p = _rope_tab(NP, 64), _rope_tab(NP, 32)
    tabA_s, tabB_s = _rope_tab(NS, 64), _rope_tab(NS, 32)
    consts = np.zeros((128, 3, 128), np.float32)
    consts[:, 0, :] = np.eye(128, dtype=np.float32)
    consts[:, 1, :] = 1.0
    consts[0:64, 2, 0:64] = 1.0
    consts[64:128, 2, 64:128] = 1.0
    ada_b0 = f(ada_b)[0]
    gq, gk = f(a_q_norm_g)[0], f(a_k_norm_g)[0]
    rot = lambda g: np.concatenate([g[32:], g[:32]])
    rows = np.concatenate([ada_b0[2048:3072], f(post_norm_g)[0]])[None, :]
    common = dict(rows=f(rows), consts=consts, ada_w=f(ada_w)[0], w_in=f(w_in)[0], b_q_up=f(b_q_up)[0], b_kv_up=f(b_kv_up)[0],
                  a_out=f(a_out)[0], b_out=f(b_out)[0], w_o=f(w_o)[0])
    in_maps = []
    for c in range(NC):
        si, hf = c // 2, c % 2
        vecs = np.zeros((128, 64), np.float32)
        vecs[:, 0:8] = f(pre_norm_g)[0].reshape(8, 128).T
        vecs[:, 8:32] = ada_b0.reshape(24, 128).T
        vecs[:, 32] = np.tile(gq, 2)
        vecs[:, 33] = np.tile(rot(gq), 2)
        vecs[:, 34] = np.tile(gk, 2)
        vecs[:, 35] = np.tile(rot(gk), 2)
        vecs[:, 36:39] = f(b_q_norm_g)[0].reshape(3, 128).T
        vecs[:, 39:41] = f(b_kv_norm_g)[0].reshape(2, 128).T
        cT = np.stack([f(c_prompt)[0], f(c_sample)[si]], axis=-1)
        vecs[:, 41:57] = cT.reshape(8, 128, 2).transpose(1, 0, 2).reshape(128, 16)
        m = dict(common)
        m["vecs"] = vecs
        m["xk0"] = np.ascontiguousarray(np.roll(xp, -c * QP, axis=0))
        m["xk1"] = np.ascontiguousarray(np.roll(xs[si], -hf * QS, axis=0))
        m["tabA0"] = np.ascontiguousarray(np.roll(tabA_p, -c * QP, axis=2))
        m["tabB0"] = np.ascontiguousarray(np.roll(tabB_p, -c * QP, axis=2))
        m["tabA1"] = np.ascontiguousarray(np.roll(tabA_s, -hf * QS, axis=2))
        m["tabB1"] = np.ascontiguousarray(np.roll(tabB_s, -hf * QS, axis=2))
        in_maps.append(m)
    y0 = [np.zeros((QP, D), np.float32) for _ in range(NC)]
    y1 = [np.zeros((QS, D), np.float32) for _ in range(NC)]
    for nc, segs in zip(ncs, launches):
        res = run_bass_kernel_spmd(nc, in_maps, core_ids=list(range(NC)))
        for (sg, qlo, qhi) in segs:
            Q = QP if sg == 0 else QS
            lo, hi = qlo * TB, (Q if qhi is None else qhi * TB)
            for c in range(NC):
                (y0 if sg == 0 else y1)[c][lo:hi] = res.results[c]["y%d" % sg][lo:hi]
    y_p = np.concatenate(y0, axis=0)[None]
    y_s = np.stack([np.concatenate([y1[2 * i], y1[2 * i + 1]], axis=0) for i in range(NC // 2)], axis=0)
    return y_p.astype(np.float32), y_s.astype(np.float32)


LAUNCHES = (((0, 0, None), (1, 0, None)),)


def kernel(**inputs):
    return _run(8, 2048, 2048, LAUNCHES, **inputs)
```
